# Optimizing a Trainium2 kernel written in Bass

```python
import jax, jax.numpy as jnp
from jax import lax
import numpy as np

D_MODEL = 1024
BATCH = 16
SEQ = 4096
DEPTH = 2
DEC_BATCH = 8
DEC_SEQ = 8192
PAST_LEN = 128

N_EVEN = (DEPTH + 1) // 2
N_ODD = DEPTH // 2
D_FF = 2816
D_MIX = D_MODEL
D_A = D_MIX // 2
D_B = D_MIX - D_A
A_HEADS = 8
A_HEAD_DIM = D_A // A_HEADS
CHUNK = 128
B_CONV_W = 3
D_CONV = D_MODEL
CONV_W = 31
D_IN_AB = 2 * D_A + 3 * D_B
RMS_EPS = 1e-6
LN_EPS = 1e-5

kernel_name = "hybrid_gmlp_shortconv_conformer_encoder"


def rmsnorm(x, g):
    xf = x.astype(jnp.float32)
    y = xf * lax.rsqrt(jnp.mean(xf * xf, axis=-1, keepdims=True) + RMS_EPS)
    return (y * g.astype(jnp.float32)).astype(x.dtype)


def layernorm(x, g, b):
    xf = x.astype(jnp.float32)
    mu = jnp.mean(xf, axis=-1, keepdims=True)
    var = jnp.mean(jnp.square(xf - mu), axis=-1, keepdims=True)
    y = (xf - mu) * lax.rsqrt(var + LN_EPS)
    return (y * g.astype(jnp.float32) + b.astype(jnp.float32)).astype(x.dtype)


def swiglu(x, w_gate, w_up, w_down):
    return (jax.nn.silu(x @ w_gate) * (x @ w_up)) @ w_down


def depthwise_conv(x, w):
    c = x.shape[-1]
    return lax.conv_general_dilated(
        x, w[:, None, :].astype(x.dtype), window_strides=(1,), padding="SAME",
        dimension_numbers=("NWC", "WIO", "NWC"), feature_group_count=c)


def chunked_spatial_gate(u, v, ws, bs):
    b, s, _ = u.shape
    shp = (b, s // CHUNK, CHUNK, A_HEADS, A_HEAD_DIM)
    vc = v.reshape(shp)
    mixed = jnp.einsum("hpq,bcqhd->bcphd", ws, vc) + bs.T[None, None, :, :, None]
    return (u.reshape(shp) * mixed).reshape(b, s, D_A)


def mixer_ab(h, w_in, a_ws, a_bs, b_conv, w_out):
    z = h @ w_in
    za = jax.nn.gelu(z[..., :2 * D_A])
    u, v = za[..., :D_A], za[..., D_A:]
    b_gate = z[..., 2 * D_A:2 * D_A + D_B]
    c_gate = z[..., 2 * D_A + D_B:2 * D_A + 2 * D_B]
    xin = z[..., 2 * D_A + 2 * D_B:]
    y_a = chunked_spatial_gate(u, v, a_ws, a_bs)
    y_b = b_gate * depthwise_conv(c_gate * xin, b_conv)
    return jnp.concatenate([y_a, y_b], axis=-1) @ w_out


def conformer_conv(h, w_pw1, b_pw1, dw_w, dw_b, ln_g, ln_b, w_pw2, b_pw2):
    z = h @ w_pw1 + b_pw1
    z = z[..., :D_CONV] * jax.nn.sigmoid(z[..., D_CONV:])
    z = depthwise_conv(z, dw_w) + dw_b
    z = jax.nn.silu(layernorm(z, ln_g, ln_b))
    return z @ w_pw2 + b_pw2


def trunk(x, ffn_norm, ffn_w_gate, ffn_w_up, ffn_w_down, mix_norm,
          ab_w_in, a_spatial_w, a_spatial_b, b_conv_w, ab_w_out,
          c_w_pw1, c_b_pw1, c_dw_w, c_dw_b, c_ln_g, c_ln_b, c_w_pw2, c_b_pw2, final_norm):
    for l in range(DEPTH):
        x = x + 0.5 * swiglu(rmsnorm(x, ffn_norm[l, 0]), ffn_w_gate[l, 0], ffn_w_up[l, 0], ffn_w_down[l, 0])
        h = rmsnorm(x, mix_norm[l])
        i = l // 2
        if l % 2 == 0:
            x = x + mixer_ab(h, ab_w_in[i], a_spatial_w[i], a_spatial_b[i], b_conv_w[i], ab_w_out[i])
        else:
            x = x + conformer_conv(h, c_w_pw1[i], c_b_pw1[i], c_dw_w[i], c_dw_b[i],
                                   c_ln_g[i], c_ln_b[i], c_w_pw2[i], c_b_pw2[i])
        x = x + 0.5 * swiglu(rmsnorm(x, ffn_norm[l, 1]), ffn_w_gate[l, 1], ffn_w_up[l, 1], ffn_w_down[l, 1])
    return rmsnorm(x, final_norm)


def setup_inputs(seed: int = 0) -> dict:
    key = jax.random.key(seed)
    ks = jax.random.split(key, 24)
    f32 = jnp.float32

    def nrm(k, shape, scale):
        return jax.random.normal(k, shape, f32) * scale

    return {
        "x_prompt": nrm(ks[0], (BATCH, SEQ, D_MODEL), 1.0),
        "x_sample": nrm(ks[1], (DEC_BATCH, DEC_SEQ, D_MODEL), 1.0),
        "ffn_norm": 1.0 + nrm(ks[2], (DEPTH, 2, D_MODEL), 0.02),
        "ffn_w_gate": nrm(ks[3], (DEPTH, 2, D_MODEL, D_FF), D_MODEL ** -0.5),
        "ffn_w_up": nrm(ks[4], (DEPTH, 2, D_MODEL, D_FF), D_MODEL ** -0.5),
        "ffn_w_down": nrm(ks[5], (DEPTH, 2, D_FF, D_MODEL), D_FF ** -0.5),
        "mix_norm": 1.0 + nrm(ks[6], (DEPTH, D_MODEL), 0.02),
        "ab_w_in": nrm(ks[7], (N_EVEN, D_MODEL, D_IN_AB), D_MODEL ** -0.5),
        "a_spatial_w": nrm(ks[8], (N_EVEN, A_HEADS, CHUNK, CHUNK), CHUNK ** -0.5),
        "a_spatial_b": 1.0 + nrm(ks[9], (N_EVEN, A_HEADS, CHUNK), 0.1),
        "b_conv_w": nrm(ks[10], (N_EVEN, B_CONV_W, D_B), B_CONV_W ** -0.5),
        "ab_w_out": nrm(ks[11], (N_EVEN, D_MIX, D_MODEL), D_MIX ** -0.5),
        "c_w_pw1": nrm(ks[12], (N_ODD, D_MODEL, 2 * D_CONV), D_MODEL ** -0.5),
        "c_b_pw1": nrm(ks[13], (N_ODD, 2 * D_CONV), 0.01),
        "c_dw_w": nrm(ks[14], (N_ODD, CONV_W, D_CONV), CONV_W ** -0.5),
        "c_dw_b": nrm(ks[15], (N_ODD, D_CONV), 0.01),
        "c_ln_g": 1.0 + nrm(ks[16], (N_ODD, D_CONV), 0.02),
        "c_ln_b": nrm(ks[17], (N_ODD, D_CONV), 0.01),
        "c_w_pw2": nrm(ks[18], (N_ODD, D_CONV, D_MODEL), D_CONV ** -0.5),
        "c_b_pw2": nrm(ks[19], (N_ODD, D_MODEL), 0.01),
        "final_norm": 1.0 + nrm(ks[20], (D_MODEL,), 0.02),
    }


def reference(x_prompt, x_sample, ffn_norm, ffn_w_gate, ffn_w_up, ffn_w_down, mix_norm,
              ab_w_in, a_spatial_w, a_spatial_b, b_conv_w, ab_w_out,
              c_w_pw1, c_b_pw1, c_dw_w, c_dw_b, c_ln_g, c_ln_b, c_w_pw2, c_b_pw2, final_norm):
    y_prompt = trunk(x_prompt, ffn_norm, ffn_w_gate, ffn_w_up, ffn_w_down, mix_norm,
                     ab_w_in, a_spatial_w, a_spatial_b, b_conv_w, ab_w_out,
                     c_w_pw1, c_b_pw1, c_dw_w, c_dw_b, c_ln_g, c_ln_b, c_w_pw2, c_b_pw2, final_norm)
    y_sample = trunk(x_sample, ffn_norm, ffn_w_gate, ffn_w_up, ffn_w_down, mix_norm,
                     ab_w_in, a_spatial_w, a_spatial_b, b_conv_w, ab_w_out,
                     c_w_pw1, c_b_pw1, c_dw_w, c_dw_b, c_ln_g, c_ln_b, c_w_pw2, c_b_pw2, final_norm)
    return (y_prompt, y_sample)
```

```python
from contextlib import ExitStack
import numpy as np
import concourse.bass as bass
import concourse.mybir as mybir
from concourse.bass_utils import run_bass_kernel_spmd

F32 = mybir.dt.float32
BF16 = mybir.dt.bfloat16
AF = mybir.ActivationFunctionType
ALU = mybir.AluOpType

D = 1024
DFF = 2816
NFC = 22
T = 512
NCORES = 8
TOK_PER_CORE = 16384
RMS_EPS = 1e-6
LN_EPS = 1e-5
NW = 4
WCOLS = 4096
NPV = 47

PV_FFN = 0
PV_MIX = 4
PV_FINAL = 6
PV_B1V = 7
PV_B1G = 8
PV_DWB = 9
PV_LNG = 10
PV_LNB = 11
PV_B2 = 12
PV_DW = 13
PV_BC = 44

def GU(k, j): return k * 19 + j
def DN(k, o): return k * 19 + 11 + o
WIN_U = 76
WIN_V = 77
def PB(c): return 78 + c
def WOUT(t): return 82 + t
def PW1(t): return 84 + t
def PW2(t): return 88 + t
NTILES = 90
A_TILES = [GU(0, j) for j in range(11)] + [DN(0, o) for o in range(8)] + [PB(c) for c in range(4)] + [WIN_V, WIN_U]
B_TILES = ([WOUT(0), WOUT(1)] + [GU(1, j) for j in range(11)] + [DN(1, o) for o in range(8)]
           + [GU(2, j) for j in range(11)] + [DN(2, o) for o in range(8)] + [PW1(t) for t in range(4)])
C_TILES = [PW2(0), PW2(1)] + [GU(3, j) for j in range(11)] + [DN(3, o) for o in range(8)]


def tile_used(idx):
    if idx < 76:
        return 4096 if (idx % 19) < 11 else NFC * 128
    if idx in (WIN_U, WIN_V):
        return 4096
    if idx < 82:
        return 3072
    return 4096


def host_weight_tiles(inputs):
    f = lambda a: np.asarray(a, dtype=np.float32)
    wg = f(inputs["ffn_w_gate"]).reshape(4, D, DFF); wu = f(inputs["ffn_w_up"]).reshape(4, D, DFF)
    wd = f(inputs["ffn_w_down"]).reshape(4, DFF, D)
    win = f(inputs["ab_w_in"]).reshape(D, 2560); wout = f(inputs["ab_w_out"]).reshape(D, D)
    pw1 = f(inputs["c_w_pw1"]).reshape(D, 2 * D); pw2 = f(inputs["c_w_pw2"]).reshape(D, D)
    out = np.zeros((NTILES, 128, WCOLS), np.float32)

    def kmaj(m):
        K, n = m.shape
        return m.reshape(K // 128, 128, n).transpose(1, 0, 2).reshape(128, (K // 128) * n)

    for k in range(4):
        for j in range(11):
            out[GU(k, j), :, 0:2048] = kmaj(wg[k][:, j * 256:(j + 1) * 256])
            out[GU(k, j), :, 2048:4096] = kmaj(wu[k][:, j * 256:(j + 1) * 256])
        for o in range(8):
            out[DN(k, o), :, 0:NFC * 128] = kmaj(wd[k][:, o * 128:(o + 1) * 128])
    out[WIN_U] = kmaj(win[:, 0:512]); out[WIN_V] = kmaj(win[:, 512:1024])
    for c in range(4):
        for w in range(3):
            out[PB(c), :, w * 1024:(w + 1) * 1024] = kmaj(win[:, 1024 + w * 512 + c * 128: 1024 + w * 512 + (c + 1) * 128])
    for t in range(2):
        out[WOUT(t)] = kmaj(wout[:, t * 512:(t + 1) * 512])
        out[PW2(t)] = kmaj(pw2[:, t * 512:(t + 1) * 512])
    for t in range(4):
        out[PW1(t), :, 0:2048] = kmaj(pw1[:, t * 256:(t + 1) * 256])
        out[PW1(t), :, 2048:4096] = kmaj(pw1[:, D + t * 256: D + (t + 1) * 256])
    return out


class Res:
    __slots__ = ("w", "r")
    def __init__(self):
        self.w = None; self.r = {}


class Eng:
    def __init__(self, name, handle, sem, inorder=False):
        self.name = name; self.h = handle; self.sem = sem; self.count = 0
        self.seen = {}; self.inorder = inorder


class Chan:
    def __init__(self, sem):
        self.sem = sem; self.count = 0


class Sched:
    def __init__(self, nc, stack):
        self.nc = nc; self.stack = stack
        mk = lambda n: stack.enter_context(nc.semaphore(n))
        self.pe = Eng("pe", nc.tensor, mk("s_pe"), inorder=True)
        self.act = Eng("act", nc.scalar, mk("s_act"))
        self.dve = Eng("dve", nc.vector, mk("s_dve"))
        self.pool = Eng("pool", nc.gpsimd, mk("s_pool"))
        self.sp = Eng("sp", nc.sync, mk("s_sp"))
        self.engs = [self.pe, self.act, self.dve, self.pool, self.sp]
        self.chans = []
        self.fence_t = stack.enter_context(nc.sbuf_tensor("fence_t", [128, 8], F32))

    def _fence(self, eng, res_list):
        f = self.fence_t
        if eng is self.act:
            emit = lambda: self.nc.scalar.copy(out=f[:, 1:2], in_=f[:, 0:1])
        elif eng is self.dve:
            emit = lambda: self.nc.vector.tensor_copy(out=f[:, 3:4], in_=f[:, 2:3])
        else:
            emit = lambda: self.nc.gpsimd.tensor_copy(out=f[:, 5:6], in_=f[:, 4:5])
        self.op(eng, emit, reads=res_list, writes=res_list)

    def chan(self, name):
        c = Chan(self.stack.enter_context(self.nc.semaphore(name)))
        self.chans.append(c); return c

    def op(self, eng, emit, reads=(), writes=(), chan=None, inc=True):
        if chan is not None:
            byeng = {}
            for r in reads:
                tok = r.w
                if tok is not None and isinstance(tok[0], Eng) and tok[0] is not self.pe and not getattr(r, "fenced", False):
                    byeng.setdefault(tok[0], []).append(r)
            for e, rl in byeng.items():
                self._fence(e, rl)
        needs = {}
        for r in reads:
            tok = r.w
            if tok is not None and needs.get(tok[0], 0) < tok[1]:
                needs[tok[0]] = tok[1]
        for w in writes:
            tok = w.w
            if tok is not None and needs.get(tok[0], 0) < tok[1]:
                needs[tok[0]] = tok[1]
            for s, c in w.r.items():
                if needs.get(s, 0) < c:
                    needs[s] = c
        for s, c in needs.items():
            if s is eng and eng.inorder:
                continue
            if eng.seen.get(s, 0) >= c:
                continue
            eng.h.wait_ge(s.sem, c); eng.seen[s] = c
        ins = emit()
        if chan is not None:
            chan.count += 16; ins.then_inc(chan.sem, 16); tok = (chan, chan.count)
        elif inc:
            eng.count += 1; ins.then_inc(eng.sem, 1); tok = (eng, eng.count)
        else:
            tok = (eng, eng.count + 1)
        for r in reads:
            if r.r.get(tok[0], 0) < tok[1]:
                r.r[tok[0]] = tok[1]
        for w in writes:
            w.w = tok; w.r = {}
        return ins

    def barrier(self):
        srcs = [e for e in self.engs if e.count > 0] + [c for c in self.chans if c.count > 0]
        for e in self.engs:
            for s in srcs:
                if s is e:
                    continue
                if e.seen.get(s, 0) >= s.count:
                    continue
                e.h.wait_ge(s.sem, s.count); e.seen[s] = s.count


class Pool:
    def __init__(self, tiles):
        self.tiles = tiles; self.res = [Res() for _ in tiles]
        self.live = [False] * len(tiles); self.nxt = 0

    def alloc(self):
        n = len(self.tiles)
        for k in range(n):
            i = (self.nxt + k) % n
            if not self.live[i]:
                self.live[i] = True; self.nxt = (i + 1) % n
                return i
        raise RuntimeError("pool exhausted")

    def free(self, i):
        assert self.live[i]; self.live[i] = False


def build_nc(nb, seq_starts, seq_ends):
    ntok = nb * T
    nc = bass.Bass("TRN2", target_bir_lowering=False)
    dt = lambda name, shape, dtp=F32, kind="ExternalInput": nc.dram_tensor(name, shape, dtp, kind=kind).ap()
    x_d = dt("x", [ntok, D])
    y_d = dt("y", [ntok, D], kind="ExternalOutput")
    wt32_d = dt("wt32", [NTILES, 128, WCOLS])
    pv_d = dt("pvec", [NPV, D]); ws_d = dt("ws", [8 * 128, 128]); bs_d = dt("bs", [1, 8 * 128])
    wsc = dt("wsc", [NTILES, 128, WCOLS], BF16, kind="Internal")

    with ExitStack() as st:
        S = Sched(nc, st)
        op = S.op
        PE, ACT, DVE, POOL, SP = S.pe, S.act, S.dve, S.pool, S.sp
        sb = lambda n, s, d: st.enter_context(nc.sbuf_tensor(n, s, d))
        ident = sb("ident", [128, 128], F32); R_ident = Res()
        ones = sb("ones", [128, 128], BF16); R_ones = Res()
        pv = sb("pv", [128, 8, 48], F32); R_pv = Res()
        wsT = sb("wsT", [128, 8, 128], BF16); R_wsT = Res()
        bhi = sb("bhi", [1, 1024], BF16); blo = sb("blo", [1, 1024], BF16); R_bias = Res()
        b2 = sb("b2", [2, 1024], BF16); R_b2 = Res()
        banks = [st.enter_context(nc.psum_tensor(f"bank{i}", [128, T], F32)) for i in range(8)]
        PS = Pool(banks)
        st.enter_context(nc.Block())
        R_fence = Res()
        op(DVE, lambda: nc.vector.memset(S.fence_t[:], 0.0), writes=[R_fence])
        op(ACT, lambda: nc.scalar.copy(out=S.fence_t[:, 1:2], in_=S.fence_t[:, 0:1]), reads=[R_fence], writes=[R_fence])
        op(POOL, lambda: nc.gpsimd.tensor_copy(out=S.fence_t[:, 5:6], in_=S.fence_t[:, 4:5]), reads=[R_fence], writes=[R_fence])
        c_misc = [S.chan("c_misc0"), S.chan("c_misc1"), S.chan("c_misc2")]
        R_wsc = {idx: Res() for idx in range(NTILES)}

        with ExitStack() as pst:
            psb = lambda n, s, d: pst.enter_context(nc.sbuf_tensor(n, s, d))
            iot = psb("iot", [128, 128], F32); R_iot = Res()
            pstg = psb("pstg", [NPV, D], F32); R_pstg = Res()
            wsl = psb("wsl", [128, 8, 128], F32); R_wsl = Res()
            bsf = psb("bsf", [1, 1024], F32); bsf2 = psb("bsf2", [1, 1024], F32); R_bsf = Res()
            op(POOL, lambda: nc.gpsimd.iota(iot[:], pattern=[[1, 128]], base=0, channel_multiplier=-1,
                                            allow_small_or_imprecise_dtypes=True), writes=[R_iot])
            op(DVE, lambda: nc.vector.tensor_single_scalar(out=ident[:], in_=iot[:], scalar=0.0, op=ALU.is_equal),
               reads=[R_iot], writes=[R_ident])
            op(DVE, lambda: nc.vector.memset(ones[:], 1.0), writes=[R_ones])
            op(SP, lambda: nc.sync.dma_start(out=pstg[:], in_=pv_d[:, :]), writes=[R_pstg], chan=c_misc[0])
            op(SP, lambda: nc.sync.dma_start(out=wsl[:], in_=ws_d.rearrange("(h p) q -> p h q", p=128)),
               writes=[R_wsl], chan=c_misc[1])
            op(SP, lambda: nc.sync.dma_start(out=bsf[:], in_=bs_d[:, :]), writes=[R_bsf], chan=c_misc[2])
            for c in range(8):
                b = PS.alloc()
                op(PE, lambda b=b, c=c: nc.tensor.transpose(banks[b][:, 0:NPV], pstg[0:NPV, c * 128:(c + 1) * 128],
                                                            ident[0:NPV, 0:NPV]),
                   reads=[R_pstg, R_ident], writes=[PS.res[b]])
                op(ACT, lambda b=b, c=c: nc.scalar.copy(out=pv[:, c, 0:NPV], in_=banks[b][:, 0:NPV]),
                   reads=[PS.res[b]], writes=[R_pv])
                PS.free(b)
            for h in range(8):
                b = PS.alloc()
                op(PE, lambda b=b, h=h: nc.tensor.transpose(banks[b][:, 0:128], wsl[:, h, :], ident[:]),
                   reads=[R_wsl, R_ident], writes=[PS.res[b]])
                op(ACT, lambda b=b, h=h: nc.scalar.copy(out=wsT[:, h, :], in_=banks[b][:, 0:128]),
                   reads=[PS.res[b]], writes=[R_wsT])
                PS.free(b)
            op(DVE, lambda: nc.vector.tensor_copy(out=bhi[:], in_=bsf[:]), reads=[R_bsf], writes=[R_bias])
            op(DVE, lambda: nc.vector.tensor_copy(out=bsf2[:], in_=bhi[:]), reads=[R_bias], writes=[R_bsf])
            op(DVE, lambda: nc.vector.tensor_tensor(out=blo[:], in0=bsf[:], in1=bsf2[:], op=ALU.subtract),
               reads=[R_bsf], writes=[R_bias])
            c_b2 = [S.chan("c_b2a"), S.chan("c_b2b")]
            op(SP, lambda: nc.sync.dma_start(out=b2[0:1, :], in_=bhi[0:1, :]), reads=[R_bias], writes=[R_b2], chan=c_b2[0])
            op(SP, lambda: nc.sync.dma_start(out=b2[1:2, :], in_=blo[0:1, :]), reads=[R_bias], writes=[R_b2], chan=c_b2[1])

            S.barrier()

        X = [sb(f"X{i}", [128, 8, T], F32) for i in range(3)]
        RX = [[Res() for _ in range(8)] for _ in range(3)]
        c_xin = [S.chan(f"c_xin{i}") for i in range(4)]
        c_xin_e = [[S.chan(f"c_xine{i}{j}") for j in range(2)] for i in range(2)]
        XOUTT = sb("XOUT", [128, 2, D], F32); XOUT = [XOUTT[:, 0, :], XOUTT[:, 1, :]]; R_XOUT = [Res(), Res()]
        c_xout = [S.chan("c_xout0"), S.chan("c_xout1")]
        H = sb("H", [128, 8, T], BF16); RH = [Res() for _ in range(8)]
        Y = [sb(f"Y{i}", [128, 8, T], BF16) for i in range(2)]; RY = [[Res() for _ in range(8)] for _ in range(2)]
        A = sb("A", [128, NFC, T], BF16); RA = [Res() for _ in range(NFC)]
        AXf = A[:, 0:16, :].rearrange("p a b -> p (a b)").bitcast(F32)
        W = [sb(f"W{i}", [128, WCOLS], BF16) for i in range(NW)]; RW = [Res() for _ in range(NW)]
        c_w = [S.chan(f"c_w{i}") for i in range(NW)]
        Z = [sb(f"Z{i}", [128, 8, T + 30], BF16) for i in range(2)]; RZ = [[Res() for _ in range(8)] for _ in range(2)]
        RZH = [[Res() for _ in range(8)] for _ in range(2)]
        HS = sb("HS", [128, 8, 15], F32); R_HS = [Res() for _ in range(8)]
        C = sb("C", [128, 8, T], F32); RC = [Res() for _ in range(8)]
        V = A; RV = RA
        TP = Pool([sb(f"TP{i}", [128, T], F32) for i in range(8)])
        SQ = Pool([sb(f"SQ{i}", [128, T], BF16) for i in range(8)])
        CX = [sb(f"CX{i}", [128, T + 2], F32) for i in range(2)]; R_CX = [Res(), Res()]
        CXP = sb("CXP", [128, 4], F32); R_CXP = [Res() for _ in range(4)]
        CXH = sb("CXH", [128, 4], F32); R_CXH = Res()
        PL = [sb(f"PL{i}", [128, 4], F32) for i in range(2)]; R_PL = [Res(), Res()]
        BL = [sb(f"BL{i}", [128, 4], F32) for i in range(2)]; R_BL = [Res(), Res()]
        FX = sb("FX", [128, 4], F32); R_FX = Res()
        WARM = sb("WARM", [128, 2], F32); R_WARM = Res(); R_WARM2 = Res()
        op(DVE, lambda: nc.vector.memset(WARM[:], 1.0), writes=[R_WARM])

        for i in range(2):
            op(DVE, lambda i=i: nc.vector.memset(Z[i][:, :, T + 15:T + 30], 0.0), writes=RZ[i])
            op(DVE, lambda i=i: nc.vector.memset(CX[i][:, T + 1:T + 2], 0.0), writes=[R_CX[i]])

        assert nb >= 3
        cv_order = [(i, 'D') for i in A_TILES]
        _cq = list(C_TILES)
        for _n, _i in enumerate(B_TILES):
            cv_order.append((_i, 'D'))
            if _n % 2 == 1 and _cq:
                cv_order.append((_cq.pop(0), 'I'))
        cv_order += [(_i, 'I') for _i in _cq]
        cv_pos = {it[0]: n for n, it in enumerate(cv_order)}
        stg32 = [X[2][:].rearrange("p a b -> p (a b)"), C[:].rearrange("p a b -> p (a b)")]
        stgR = [RX[2], RC]
        stb16 = [Z[1][:].rearrange("p a b -> p (a b)")[:, 0:WCOLS], XOUTT[:].rearrange("p a b -> p (a b)").bitcast(BF16)]
        stbR = [RZ[1] + RZH[1], R_XOUT]
        c_cvl = [S.chan("c_cvl0"), S.chan("c_cvl1")]
        c_cvs = [S.chan("c_cvs0"), S.chan("c_cvs1")]
        c_cvd = [S.chan(f"c_cvd{i}") for i in range(NW)]
        cv = {"loaded": 0, "done": 0, "tick": 0, "ni": 0}

        def cv_load(n):
            idx = cv_order[n][0]; used = tile_used(idx); k = n % 2
            op(POOL, lambda: nc.gpsimd.dma_start(out=stg32[k][:, 0:used], in_=wt32_d[idx, :, 0:used]), writes=stgR[k], chan=c_cvl[k])

        def cv_prefetch():
            while cv["loaded"] < min(cv["done"] + 2, len(cv_order)):
                cv_load(cv["loaded"]); cv["loaded"] += 1

        def cv_indirect():
            n = cv["done"]; idx, mode = cv_order[n]
            assert mode == 'I'
            cv_prefetch()
            used = tile_used(idx); k = n % 2; j = cv["ni"] % 2; cv["ni"] += 1
            op(DVE, lambda: nc.vector.tensor_copy(out=stb16[j][:, 0:used], in_=stg32[k][:, 0:used]), reads=stgR[k], writes=stbR[j])
            op(POOL, lambda: nc.gpsimd.dma_start(out=wsc[idx, :, 0:used], in_=stb16[j][:, 0:used]), reads=stbR[j],
               writes=[R_wsc[idx]], chan=c_cvs[j])
            cv["done"] += 1
            cv_prefetch()

        def cv_direct(idx, slot):
            while cv_order[cv["done"]][0] != idx:
                cv_indirect()
            n = cv["done"]
            assert cv_order[n] == (idx, 'D')
            cv_prefetch()
            used = tile_used(idx); k = n % 2
            op(DVE, lambda: nc.vector.tensor_copy(out=W[slot][:, 0:used], in_=stg32[k][:, 0:used]), reads=stgR[k], writes=[RW[slot]])
            op(POOL, lambda: nc.gpsimd.dma_start(out=wsc[idx, :, 0:used], in_=W[slot][:, 0:used]), reads=[RW[slot]],
               writes=[R_wsc[idx]], chan=c_cvd[slot])
            cv["done"] += 1
            cv_prefetch()

        def cv_background():
            if cv["done"] < len(cv_order) and cv_order[cv["done"]][1] == 'I':
                cv_indirect()

        def cv_ensure(idx):
            pos = cv_pos[idx]
            while cv["done"] <= pos:
                cv_indirect()

        def cv_finish():
            while cv["done"] < len(cv_order):
                cv_indirect()
            op(DVE, lambda: nc.vector.memset(Z[1][:, :, T + 15:T + 30], 0.0), writes=RZ[1])

        A_T = A_TILES
        B_T = B_TILES
        C_T = C_TILES
        plan = []
        for s in range(nb + 2):
            if s < nb: plan += A_T
            if 0 <= s - 1 < nb: plan += B_T
            if 0 <= s - 2 < nb: plan += C_T
        TILE_USED = {idx: tile_used(idx) for idx in range(NTILES)}
        wstate = {"issued": 0, "used": 0}

        def w_issue_upto(n):
            while wstate["issued"] < min(n, len(plan)):
                i = wstate["issued"]; slot = i % NW; idx = plan[i]; used = TILE_USED[idx]
                pos = cv_pos[idx]
                if pos >= cv["done"] and cv_order[pos][1] == 'D':
                    cv_direct(idx, slot)
                else:
                    cv_ensure(idx)
                    op(SP, lambda slot=slot, idx=idx, used=used: nc.sync.dma_start(out=W[slot][:, 0:used], in_=wsc[idx, :, 0:used]),
                       reads=[R_wsc[idx]], writes=[RW[slot]], chan=c_w[slot])
                wstate["issued"] += 1

        def wget(idx):
            i = wstate["used"]
            assert plan[i] == idx, (i, plan[i], idx)
            w_issue_upto(i + NW - 1)
            wstate["used"] += 1
            slot = i % NW
            return W[slot], RW[slot]

        bg = []

        def pump(n):
            cv["tick"] += 1
            if cv["tick"] % 3 == 0:
                cv_background()
            for _ in range(n):
                if not bg:
                    return
                bg.pop(0)()

        def drain():
            while bg:
                bg.pop(0)()

        def mm_group(bank, lhs_list, rhs_list, reads, per=None):
            n = len(lhs_list)
            o = banks[bank][:]
            for i in range(n):
                rd = reads if per is None else reads + [per[i]]
                op(PE, lambda i=i: nc.tensor.matmul(o, lhs_list[i], rhs_list[i], start=(i == 0), stop=(i == n - 1)),
                   reads=rd, writes=[PS.res[bank]], inc=(i == n - 1))

        def mm_multi(groups):
            for i in range(8):
                for (bank, lhs_list, rhs_list, reads, per) in groups:
                    op(PE, lambda: nc.tensor.matmul(banks[bank][:], lhs_list[i], rhs_list[i], start=(i == 0), stop=(i == 7)),
                       reads=reads + [per[i]], writes=[PS.res[bank]], inc=(i == 7))

        def rstd_from_bank(sbank, eps):
            r = TP.alloc()
            op(ACT, lambda: nc.scalar.activation(out=TP.tiles[r][:], in_=banks[sbank][:], func=AF.Ln, bias=eps, scale=1.0 / D),
               reads=[PS.res[sbank]], writes=[TP.res[r]])
            PS.free(sbank)
            op(ACT, lambda: nc.scalar.activation(out=TP.tiles[r][:], in_=TP.tiles[r][:], func=AF.Exp, scale=-0.5),
               reads=[TP.res[r]], writes=[TP.res[r]])
            return r

        def warm_sqrt():
            op(ACT, lambda: nc.scalar.activation(out=WARM[:, 1:2], in_=WARM[:, 0:1], func=AF.Ln), reads=[R_WARM], writes=[R_WARM2])

        def rms_to_H(xb, rx, grow):
            warm_sqrt()
            sbank = PS.alloc()
            for c in range(8):
                q = SQ.alloc()
                op(ACT, lambda c=c, q=q: nc.scalar.activation(out=SQ.tiles[q][:], in_=xb[:, c, :], func=AF.Square),
                   reads=[rx[c]], writes=[SQ.res[q]])
                op(PE, lambda c=c, q=q: nc.tensor.matmul(banks[sbank][:], ones[:], SQ.tiles[q][:], start=(c == 0), stop=(c == 7)),
                   reads=[R_ones, SQ.res[q]], writes=[PS.res[sbank]], inc=True)
                SQ.free(q)
            r = rstd_from_bank(sbank, RMS_EPS)
            for c in range(8):
                op(DVE, lambda c=c: nc.vector.scalar_tensor_tensor(out=H[:, c, :], in0=xb[:, c, :], scalar=pv[:, c, grow:grow + 1],
                                                                   in1=TP.tiles[r][:], op0=ALU.mult, op1=ALU.mult),
                   reads=[rx[c], TP.res[r], R_pv], writes=[RH[c]])
            TP.free(r)

        def ffn(k, xb, rx, skip_norm=False):
            if not skip_norm:
                rms_to_H(xb, rx, PV_FFN + k)
            def evac_pair(f, bg_, bu_):
                sg = TP.alloc()
                op(ACT, lambda: nc.scalar.activation(out=TP.tiles[sg][:], in_=banks[bg_][:], func=AF.Silu),
                   reads=[PS.res[bg_]], writes=[TP.res[sg]])
                op(DVE, lambda: nc.vector.tensor_tensor(out=A[:, f, :], in0=TP.tiles[sg][:], in1=banks[bu_][:], op=ALU.mult),
                   reads=[TP.res[sg], PS.res[bu_]], writes=[RA[f]])
                TP.free(sg); PS.free(bg_); PS.free(bu_)

            hk = [H[:, kc, :] for kc in range(8)]
            w0, rw0 = wget(GU(k, 0)); w1, rw1 = wget(GU(k, 1))
            groups = []; pairs = []
            for (wt, rw, j) in ((w0, rw0, 0), (w1, rw1, 1)):
                for fp in range(2):
                    bg_ = PS.alloc(); bu_ = PS.alloc()
                    groups.append((bg_, [wt[:, kc * 256 + fp * 128: kc * 256 + fp * 128 + 128] for kc in range(8)], hk, [rw], RH))
                    groups.append((bu_, [wt[:, 2048 + kc * 256 + fp * 128: 2048 + kc * 256 + fp * 128 + 128] for kc in range(8)], hk, [rw], RH))
                    pairs.append((2 * j + fp, bg_, bu_))
            mm_multi(groups)
            for (f, bg_, bu_) in pairs:
                evac_pair(f, bg_, bu_)
                pump(4)
            for j in range(2, 11):
                wt, rw = wget(GU(k, j))
                for fp in range(2):
                    f = 2 * j + fp
                    bg_ = PS.alloc(); bu_ = PS.alloc()
                    mm_group(bg_, [wt[:, kc * 256 + fp * 128: kc * 256 + fp * 128 + 128] for kc in range(8)], hk, reads=[rw], per=RH)
                    mm_group(bu_, [wt[:, 2048 + kc * 256 + fp * 128: 2048 + kc * 256 + fp * 128 + 128] for kc in range(8)], hk, reads=[rw], per=RH)
                    evac_pair(f, bg_, bu_)
                    pump(4)
            for o in range(8):
                wt, rw = wget(DN(k, o))
                bo = PS.alloc()
                mm_group(bo, [wt[:, fc * 128:(fc + 1) * 128] for fc in range(NFC)], [A[:, fc, :] for fc in range(NFC)],
                         reads=[rw], per=RA)
                op(DVE, lambda: nc.vector.scalar_tensor_tensor(out=xb[:, o, :], in0=banks[bo][:], scalar=0.5, in1=xb[:, o, :],
                                                               op0=ALU.mult, op1=ALU.add),
                   reads=[PS.res[bo], rx[o]], writes=[rx[o]])
                PS.free(bo)
                if o < 6:
                    pump(7)

        def x_src(b, s4):
            if s4 < 2:
                v = Y[b % 2][:, 4 * s4:4 * s4 + 4, :].rearrange("p a b -> p (a b)").bitcast(F32)
                return v, RY[b % 2][4 * s4:4 * s4 + 4]
            k = s4 - 2
            return AXf[:, k * 1024:(k + 1) * 1024], RA[4 * k:4 * k + 4]

        def load_x_early(b):
            for s4 in range(2):
                v, rr = x_src(b, s4)
                r0 = b * T + s4 * 128
                op(POOL, lambda: nc.gpsimd.dma_start(out=v, in_=x_d[r0:r0 + 128, :]), writes=rr, chan=c_xin_e[b % 2][s4])

        def load_x_late(b):
            for s4 in range(2, 4):
                v, rr = x_src(b, s4)
                r0 = b * T + s4 * 128
                op(ACT, lambda: nc.scalar.dma_start(out=v, in_=x_d[r0:r0 + 128, :]), writes=rr, chan=c_xin[s4])

        def stage_A_head(b):
            xb = X[b % 3]; rx = RX[b % 3]
            for s4 in range(4):
                v, rr = x_src(b, s4)
                for half in range(2):
                    bk = PS.alloc()
                    for cc in range(4):
                        c = half * 4 + cc
                        op(PE, lambda: nc.tensor.transpose(banks[bk][:, cc * 128:(cc + 1) * 128], v[:, c * 128:(c + 1) * 128], ident[:]),
                           reads=rr + [R_ident], writes=[PS.res[bk]], inc=(cc == 3))
                    o_ap = xb[:, half * 4:half * 4 + 4, s4 * 128:(s4 + 1) * 128]
                    i_ap = banks[bk][:].rearrange("p (a b) -> p a b", b=128)
                    if half == 0:
                        op(ACT, lambda: nc.scalar.copy(out=o_ap, in_=i_ap), reads=[PS.res[bk]], writes=rx[half * 4:half * 4 + 4])
                    else:
                        op(DVE, lambda: nc.vector.tensor_copy(out=o_ap, in_=i_ap), reads=[PS.res[bk]], writes=rx[half * 4:half * 4 + 4])
                    PS.free(bk)
            rms_to_H(xb, rx, PV_FFN + 0)

        def stage_A(b):
            xb = X[b % 3]; rx = RX[b % 3]; yb = Y[b % 2]; ry = RY[b % 2]
            ffn(0, xb, rx, skip_norm=True)
            rms_to_H(xb, rx, PV_MIX + 0)
            for c in range(4):
                wt, rw = wget(PB(c))
                bb = PS.alloc(); bc = PS.alloc(); bx = PS.alloc()
                for (bk, wsel) in ((bc, 1), (bx, 2), (bb, 0)):
                    mm_group(bk, [wt[:, wsel * 1024 + kc * 128: wsel * 1024 + (kc + 1) * 128] for kc in range(8)],
                             [H[:, kc, :] for kc in range(8)], reads=[rw], per=RH)
                xs = TP.alloc()
                op(ACT, lambda: nc.scalar.copy(out=TP.tiles[xs][:], in_=banks[bx][:]), reads=[PS.res[bx]], writes=[TP.res[xs]])
                PS.free(bx)
                cx = CX[c % 2]; rcx = R_CX[c % 2]
                op(DVE, lambda: nc.vector.tensor_tensor(out=cx[:, 1:T + 1], in0=banks[bc][:], in1=TP.tiles[xs][:], op=ALU.mult),
                   reads=[PS.res[bc], TP.res[xs]], writes=[rcx])
                TP.free(xs); PS.free(bc)
                if b in seq_starts:
                    op(DVE, lambda: nc.vector.memset(cx[:, 0:1], 0.0), writes=[rcx])
                else:
                    op(DVE, lambda: nc.vector.tensor_copy(out=cx[:, 0:1], in_=CXP[:, c:c + 1]), reads=[R_CXP[c]], writes=[rcx])
                t1 = TP.alloc(); tt = TP.tiles[t1]
                op(DVE, lambda: nc.vector.tensor_scalar_mul(out=tt[:], in0=cx[:, 0:T], scalar1=pv[:, c, PV_BC:PV_BC + 1]),
                   reads=[rcx, R_pv], writes=[TP.res[t1]])
                op(DVE, lambda: nc.vector.scalar_tensor_tensor(out=tt[:], in0=cx[:, 1:T + 1], scalar=pv[:, c, PV_BC + 1:PV_BC + 2], in1=tt[:],
                                                               op0=ALU.mult, op1=ALU.add),
                   reads=[rcx, R_pv, TP.res[t1]], writes=[TP.res[t1]])
                op(DVE, lambda: nc.vector.scalar_tensor_tensor(out=tt[:], in0=cx[:, 2:T + 2], scalar=pv[:, c, PV_BC + 2:PV_BC + 3], in1=tt[:],
                                                               op0=ALU.mult, op1=ALU.add),
                   reads=[rcx, R_pv, TP.res[t1]], writes=[TP.res[t1]])
                op(DVE, lambda: nc.vector.tensor_copy(out=PL[b % 2][:, c:c + 1], in_=tt[:, T - 1:T]), reads=[TP.res[t1]], writes=[R_PL[b % 2]])
                op(DVE, lambda: nc.vector.tensor_copy(out=BL[b % 2][:, c:c + 1], in_=banks[bb][:, T - 1:T]), reads=[PS.res[bb]], writes=[R_BL[b % 2]])
                op(DVE, lambda: nc.vector.tensor_copy(out=CXP[:, c:c + 1], in_=cx[:, T:T + 1]), reads=[rcx], writes=[R_CXP[c]])
                op(DVE, lambda: nc.vector.tensor_copy(out=CXH[:, c:c + 1], in_=cx[:, 1:2]), reads=[rcx], writes=[R_CXH])
                op(DVE, lambda: nc.vector.tensor_tensor(out=yb[:, 4 + c, :], in0=tt[:], in1=banks[bb][:], op=ALU.mult),
                   reads=[TP.res[t1], PS.res[bb]], writes=[ry[4 + c]])
                TP.free(t1); PS.free(bb)
            wv, rwv = wget(WIN_V)
            wu, rwu = wget(WIN_U)
            vb = [PS.alloc() for _ in range(4)]
            mm_multi([(vb[s4], [H[:, kc, s4 * 128:(s4 + 1) * 128] for kc in range(8)], [wv[:, kc * 512:(kc + 1) * 512] for kc in range(8)],
                       [rwv], RH) for s4 in range(4)])
            for s4 in range(4):
                bk = vb[s4]
                op(ACT, lambda: nc.scalar.activation(out=V[:, s4, :], in_=banks[bk][:], func=AF.Gelu),
                   reads=[PS.res[bk]], writes=[RV[s4]])
                PS.free(bk)
            mb = [PS.alloc() for _ in range(4)]
            for s4 in range(4):
                for hc in range(4):
                    for hh in range(2):
                        h = 2 * hc + hh
                        o = banks[mb[hc]][hh * 64:(hh + 1) * 64, s4 * 128:(s4 + 1) * 128]
                        op(PE, lambda: nc.tensor.matmul(o, ones[0:2, 0:64], b2[0:2, h * 128:(h + 1) * 128], start=(s4 == 0), stop=False),
                           reads=[R_ones, R_b2], writes=[PS.res[mb[hc]]], inc=False)
            ugs = []
            for c in range(4):
                bk = PS.alloc()
                mm_group(bk, [wu[:, kc * 512 + c * 128: kc * 512 + (c + 1) * 128] for kc in range(8)], [H[:, kc, :] for kc in range(8)],
                         reads=[rwu], per=RH)
                ug = TP.alloc()
                op(ACT, lambda: nc.scalar.activation(out=TP.tiles[ug][:], in_=banks[bk][:], func=AF.Gelu),
                   reads=[PS.res[bk]], writes=[TP.res[ug]])
                PS.free(bk)
                ugs.append(ug)
            for s4 in range(4):
                for hc in range(4):
                    for hh in range(2):
                        h = 2 * hc + hh
                        o = banks[mb[hc]][hh * 64:(hh + 1) * 64, s4 * 128:(s4 + 1) * 128]
                        op(PE, lambda: nc.tensor.matmul(o, V[:, s4, h * 64:(h + 1) * 64], wsT[:, h, :], start=False, stop=(s4 == 3)),
                           reads=[RV[s4], R_wsT], writes=[PS.res[mb[hc]]], inc=(s4 == 3 and hh == 1))
            for c in range(4):
                ug = ugs[c]
                op(DVE, lambda: nc.vector.tensor_tensor(out=yb[:, c, :], in0=TP.tiles[ug][:], in1=banks[mb[c]][:], op=ALU.mult),
                   reads=[TP.res[ug], PS.res[mb[c]]], writes=[ry[c]])
                TP.free(ug); PS.free(mb[c])

        def stage_B(b):
            xb = X[b % 3]; rx = RX[b % 3]; yb = Y[b % 2]; ry = RY[b % 2]
            zb = Z[b % 2]; rz = RZ[b % 2]
            if b not in seq_ends:
                op(DVE, lambda: nc.vector.tensor_tensor(out=FX[:], in0=CXH[:], in1=pv[:, 0:4, PV_BC + 2], op=ALU.mult),
                   reads=[R_CXH, R_pv], writes=[R_FX])
                op(DVE, lambda: nc.vector.tensor_tensor(out=FX[:], in0=FX[:], in1=PL[b % 2][:], op=ALU.add),
                   reads=[R_FX, R_PL[b % 2]], writes=[R_FX])
                op(DVE, lambda: nc.vector.tensor_tensor(out=yb[:, 4:8, T - 1], in0=FX[:], in1=BL[b % 2][:], op=ALU.mult),
                   reads=[R_FX, R_BL[b % 2]], writes=ry[4:8])
            ykc = [yb[:, kc, :] for kc in range(8)]
            for t in range(2):
                wt, rw = wget(WOUT(t))
                bks = [PS.alloc() for _ in range(4)]
                grp = [(bks[oc], [wt[:, kc * 512 + oc * 128: kc * 512 + (oc + 1) * 128] for kc in range(8)], ykc, [rw], ry) for oc in range(4)]
                if t == 0:
                    mm_multi(grp)
                else:
                    for g in grp:
                        mm_group(g[0], g[1], g[2], reads=g[3], per=g[4])
                for oc in range(4):
                    o = 4 * t + oc; bk = bks[oc]
                    op(DVE, lambda: nc.vector.tensor_tensor(out=xb[:, o, :], in0=banks[bk][:], in1=xb[:, o, :], op=ALU.add),
                       reads=[PS.res[bk], rx[o]], writes=[rx[o]])
                    PS.free(bk)
            if b + 2 < nb:
                load_x_early(b + 2)
            ffn(1, xb, rx)
            ffn(2, xb, rx)
            rms_to_H(xb, rx, PV_MIX + 1)
            bc = b - 1
            has_prev = bc >= 0
            do_tail = has_prev and (bc not in seq_ends)
            rzh = RZH[b % 2]
            if has_prev:
                drain()
                ln_banks[bc] = (PS.alloc(), PS.alloc())
            hb = [PS.alloc(), PS.alloc()]
            hk = [H[:, kc, :] for kc in range(8)]
            hk15 = [H[:, kc, 0:15] for kc in range(8)]
            tiles = {}

            def heads(t):
                wt, rw = tiles[t]
                bank = hb[t % 2]
                for cc in range(2):
                    for (vg, off) in ((0, 0), (1, 256)):
                        o = banks[bank][:, off + cc * 16: off + cc * 16 + 15]
                        for kc in range(8):
                            lw = wt[:, vg * 2048 + kc * 256 + cc * 128: vg * 2048 + kc * 256 + (cc + 1) * 128]
                            op(PE, lambda: nc.tensor.matmul(o, lw, hk15[kc], start=(kc == 0), stop=(kc == 7)),
                               reads=[rw, RH[kc]], writes=[PS.res[bank]], inc=(kc == 7))
                for cc in range(2):
                    c = 2 * t + cc
                    op(ACT, lambda: nc.scalar.activation(out=HS[:, c, :], in_=banks[bank][:, 256 + cc * 16: 256 + cc * 16 + 15],
                                                         func=AF.Sigmoid, bias=pv[:, c, PV_B1G:PV_B1G + 1]),
                       reads=[PS.res[bank], R_pv], writes=[R_HS[c]])
                    op(DVE, lambda: nc.vector.scalar_tensor_tensor(out=zb[:, c, 15:30], in0=banks[bank][:, cc * 16: cc * 16 + 15],
                                                                   scalar=pv[:, c, PV_B1V:PV_B1V + 1], in1=HS[:, c, :],
                                                                   op0=ALU.add, op1=ALU.mult),
                       reads=[PS.res[bank], R_HS[c], R_pv], writes=[rzh[c]])

            def tails(t):
                if not do_tail:
                    return
                for j in range(16, 31):
                    n = j - 15
                    for c in (2 * t, 2 * t + 1):
                        op(DVE, lambda: nc.vector.scalar_tensor_tensor(
                            out=C[:, c, T - n:T], in0=zb[:, c, 15:15 + n], scalar=pv[:, c, PV_DW + j:PV_DW + j + 1], in1=C[:, c, T - n:T],
                            op0=ALU.mult, op1=ALU.add), reads=[rzh[c], R_pv, RC[c]], writes=[RC[c]])

            tiles[0] = wget(PW1(0)); heads(0); tails(0)
            for t in range(4):
                if t < 3:
                    tiles[t + 1] = wget(PW1(t + 1)); heads(t + 1)
                wt, rw = tiles[t]
                for cc in range(2):
                    c = 2 * t + cc
                    bv = PS.alloc(); bgt = PS.alloc()
                    mm_group(bv, [wt[:, kc * 256 + cc * 128: kc * 256 + (cc + 1) * 128] for kc in range(8)], hk, reads=[rw], per=RH)
                    mm_group(bgt, [wt[:, 2048 + kc * 256 + cc * 128: 2048 + kc * 256 + (cc + 1) * 128] for kc in range(8)], hk,
                             reads=[rw], per=RH)
                    sg = TP.alloc()
                    op(ACT, lambda: nc.scalar.activation(out=TP.tiles[sg][:], in_=banks[bgt][:], func=AF.Sigmoid, bias=pv[:, c, PV_B1G:PV_B1G + 1]),
                       reads=[PS.res[bgt], R_pv], writes=[TP.res[sg]])
                    op(DVE, lambda: nc.vector.scalar_tensor_tensor(out=zb[:, c, 30:15 + T], in0=banks[bv][:, 15:T], scalar=pv[:, c, PV_B1V:PV_B1V + 1],
                                                                   in1=TP.tiles[sg][:, 15:T], op0=ALU.add, op1=ALU.mult),
                       reads=[PS.res[bv], TP.res[sg], R_pv], writes=[rz[c]])
                    TP.free(sg); PS.free(bv); PS.free(bgt)
                if t < 3:
                    tails(t + 1)
                ln_flush(2)
                if has_prev:
                    for cc in range(2):
                        ln_prep_chunk(bc, 2 * t + cc, None)
            ln_flush(0)
            PS.free(hb[0]); PS.free(hb[1])
            if has_prev:
                warm_sqrt()
            if b in seq_starts:
                op(DVE, lambda: nc.vector.memset(zb[:, :, 0:15], 0.0), writes=rz)
            else:
                zp = Z[(b - 1) % 2]
                op(DVE, lambda: nc.vector.tensor_copy(out=zb[:, :, 0:15], in_=zp[:, :, T:T + 15]), reads=RZ[(b - 1) % 2], writes=rz)

        def enqueue_conv(b):
            zb = Z[b % 2]; rz = RZ[b % 2]; rzh = RZH[b % 2]
            for cp in range(4):
                for j in range(31):
                    for c in (2 * cp, 2 * cp + 1):
                        if j == 0:
                            bg.append(lambda c=c: op(DVE, lambda: nc.vector.tensor_scalar(
                                out=C[:, c, :], in0=zb[:, c, 0:T], scalar1=pv[:, c, PV_DW:PV_DW + 1], scalar2=pv[:, c, PV_DWB:PV_DWB + 1],
                                op0=ALU.mult, op1=ALU.add), reads=[rz[c], rzh[c], R_pv], writes=[RC[c]]))
                        else:
                            bg.append(lambda c=c, j=j: op(DVE, lambda: nc.vector.scalar_tensor_tensor(
                                out=C[:, c, :], in0=zb[:, c, j:j + T], scalar=pv[:, c, PV_DW + j:PV_DW + j + 1], in1=C[:, c, :],
                                op0=ALU.mult, op1=ALU.add), reads=[rz[c], rzh[c], R_pv, RC[c]], writes=[RC[c]]))

        ln_banks = {}

        def ln_prep_chunk(bc, c, tail_from):
            if tail_from is not None:
                zn, rzn = tail_from
                for j in range(16, 31):
                    n = j - 15
                    op(DVE, lambda j=j, n=n: nc.vector.scalar_tensor_tensor(
                        out=C[:, c, T - n:T], in0=zn[:, c, 15:15 + n], scalar=pv[:, c, PV_DW + j:PV_DW + j + 1], in1=C[:, c, T - n:T],
                        op0=ALU.mult, op1=ALU.add), reads=[rzn[c], R_pv, RC[c]], writes=[RC[c]])
            q1 = SQ.alloc()
            op(ACT, lambda: nc.scalar.copy(out=SQ.tiles[q1][:], in_=C[:, c, :]), reads=[RC[c]], writes=[SQ.res[q1]])
            q2 = SQ.alloc()
            op(ACT, lambda: nc.scalar.activation(out=SQ.tiles[q2][:], in_=C[:, c, :], func=AF.Square), reads=[RC[c]], writes=[SQ.res[q2]])
            ln_pending.append((bc, c, q1, q2))

        ln_pending = []

        def ln_flush(keep):
            while len(ln_pending) > keep:
                bc, c, q1, q2 = ln_pending.pop(0)
                s1, s2 = ln_banks[bc]
                op(PE, lambda: nc.tensor.matmul(banks[s1][:], ones[:], SQ.tiles[q1][:], start=(c == 0), stop=(c == 7)),
                   reads=[R_ones, SQ.res[q1]], writes=[PS.res[s1]], inc=True)
                op(PE, lambda: nc.tensor.matmul(banks[s2][:], ones[:], SQ.tiles[q2][:], start=(c == 0), stop=(c == 7)),
                   reads=[R_ones, SQ.res[q2]], writes=[PS.res[s2]], inc=True)
                SQ.free(q1); SQ.free(q2)

        def stage_C(b):
            xb = X[b % 3]; rx = RX[b % 3]
            drain()
            if b not in ln_banks:
                assert b in seq_ends
                ln_banks[b] = (PS.alloc(), PS.alloc())
                for c in range(8):
                    ln_prep_chunk(b, c, None)
                    ln_flush(2)
                ln_flush(0)
            s1, s2 = ln_banks.pop(b)
            mean = TP.alloc(); var = TP.alloc()
            op(ACT, lambda: nc.scalar.activation(out=TP.tiles[var][:], in_=banks[s1][:], func=AF.Square, scale=1.0 / D),
               reads=[PS.res[s1]], writes=[TP.res[var]])
            op(ACT, lambda: nc.scalar.mul(out=TP.tiles[mean][:], in_=banks[s1][:], mul=1.0 / D), reads=[PS.res[s1]], writes=[TP.res[mean]])
            PS.free(s1)
            op(DVE, lambda: nc.vector.scalar_tensor_tensor(out=TP.tiles[var][:], in0=banks[s2][:], scalar=1.0 / D, in1=TP.tiles[var][:],
                                                           op0=ALU.mult, op1=ALU.subtract),
               reads=[PS.res[s2], TP.res[var]], writes=[TP.res[var]])
            PS.free(s2)
            op(ACT, lambda: nc.scalar.activation(out=TP.tiles[var][:], in_=TP.tiles[var][:], func=AF.Ln, bias=LN_EPS, scale=1.0),
               reads=[TP.res[var]], writes=[TP.res[var]])
            op(ACT, lambda: nc.scalar.activation(out=TP.tiles[var][:], in_=TP.tiles[var][:], func=AF.Exp, scale=-0.5),
               reads=[TP.res[var]], writes=[TP.res[var]])
            for c in range(8):
                op(DVE, lambda: nc.vector.tensor_tensor(out=C[:, c, :], in0=C[:, c, :], in1=TP.tiles[mean][:], op=ALU.subtract),
                   reads=[RC[c], TP.res[mean]], writes=[RC[c]])
                op(DVE, lambda: nc.vector.tensor_tensor(out=C[:, c, :], in0=C[:, c, :], in1=TP.tiles[var][:], op=ALU.mult),
                   reads=[RC[c], TP.res[var]], writes=[RC[c]])
                op(ACT, lambda: nc.scalar.activation(out=H[:, c, :], in_=C[:, c, :], func=AF.Silu, bias=pv[:, c, PV_LNB:PV_LNB + 1],
                                                     scale=pv[:, c, PV_LNG:PV_LNG + 1]),
                   reads=[RC[c], R_pv], writes=[RH[c]])
            TP.free(mean); TP.free(var)
            hk = [H[:, kc, :] for kc in range(8)]
            for t in range(2):
                wt, rw = wget(PW2(t))
                bks = [PS.alloc() for _ in range(4)]
                grp = [(bks[oc], [wt[:, kc * 512 + oc * 128: kc * 512 + (oc + 1) * 128] for kc in range(8)], hk, [rw], RH) for oc in range(4)]
                if t == 0:
                    mm_multi(grp)
                else:
                    for g in grp:
                        mm_group(g[0], g[1], g[2], reads=g[3], per=g[4])
                for oc in range(4):
                    o = 4 * t + oc; bk = bks[oc]
                    op(DVE, lambda: nc.vector.scalar_tensor_tensor(out=xb[:, o, :], in0=banks[bk][:], scalar=pv[:, o, PV_B2:PV_B2 + 1],
                                                                   in1=xb[:, o, :], op0=ALU.add, op1=ALU.add),
                       reads=[PS.res[bk], rx[o], R_pv], writes=[rx[o]])
                    PS.free(bk)
            ffn(3, xb, rx)
            warm_sqrt()
            sbank = PS.alloc()
            for c in range(8):
                q = SQ.alloc()
                op(ACT, lambda: nc.scalar.activation(out=SQ.tiles[q][:], in_=xb[:, c, :], func=AF.Square), reads=[rx[c]], writes=[SQ.res[q]])
                op(PE, lambda: nc.tensor.matmul(banks[sbank][:], ones[:], SQ.tiles[q][:], start=(c == 0), stop=(c == 7)),
                   reads=[R_ones, SQ.res[q]], writes=[PS.res[sbank]], inc=True)
                SQ.free(q)
            r = rstd_from_bank(sbank, RMS_EPS)
            for c in range(8):
                op(DVE, lambda: nc.vector.scalar_tensor_tensor(out=C[:, c, :], in0=xb[:, c, :], scalar=pv[:, c, PV_FINAL:PV_FINAL + 1],
                                                               in1=TP.tiles[r][:], op0=ALU.mult, op1=ALU.mult),
                   reads=[rx[c], TP.res[r], R_pv], writes=[RC[c]])
            TP.free(r)

        def stage_C_out(b):
            for s4 in range(4):
                xo = XOUT[s4 % 2]; rxo = R_XOUT[s4 % 2]
                for half in range(2):
                    bk = PS.alloc()
                    for cc in range(4):
                        c = half * 4 + cc
                        op(PE, lambda: nc.tensor.transpose(banks[bk][:, cc * 128:(cc + 1) * 128], C[:, c, s4 * 128:(s4 + 1) * 128], ident[:]),
                           reads=[RC[c], R_ident], writes=[PS.res[bk]], inc=(cc == 3))
                    op(ACT, lambda: nc.scalar.copy(out=xo[:, half * 512:(half + 1) * 512], in_=banks[bk][:]),
                       reads=[PS.res[bk]], writes=[rxo])
                    PS.free(bk)
                r0 = b * T + s4 * 128
                op(POOL, lambda: nc.gpsimd.dma_start(out=y_d[r0:r0 + 128, :], in_=xo[:]), reads=[rxo], chan=c_xout[s4 % 2])

        load_x_early(0); load_x_late(0)
        if nb > 1:
            load_x_early(1)
        stage_A_head(0)
        for s in range(nb + 2):
            if 0 <= s - 2 < nb:
                enqueue_conv(s - 2)
            if s < nb:
                stage_A(s)
            if 0 <= s - 1 < nb:
                stage_B(s - 1)
            if 0 <= s - 2 < nb:
                stage_C(s - 2)
            if s == 1:
                cv_finish()
            if s + 1 < nb:
                load_x_late(s + 1)
                stage_A_head(s + 1)
            if 0 <= s - 2 < nb:
                stage_C_out(s - 2)
        drain()
        assert wstate["used"] == len(plan)
        for c in c_xout:
            nc.gpsimd.wait_ge(c.sem, c.count)
        S.barrier()
    return nc


def make_host_inputs(inputs):
    f = lambda a: np.ascontiguousarray(np.asarray(a, dtype=np.float32))
    pvec = np.zeros((NPV, D), np.float32)
    pvec[0:4] = f(inputs["ffn_norm"]).reshape(4, D)
    pvec[4:6] = f(inputs["mix_norm"]).reshape(2, D)
    pvec[6] = f(inputs["final_norm"]).reshape(D)
    pvec[7:9] = f(inputs["c_b_pw1"]).reshape(2, D)
    pvec[9] = f(inputs["c_dw_b"]).reshape(D)
    pvec[10] = f(inputs["c_ln_g"]).reshape(D)
    pvec[11] = f(inputs["c_ln_b"]).reshape(D)
    pvec[12] = f(inputs["c_b_pw2"]).reshape(D)
    pvec[13:44] = f(inputs["c_dw_w"]).reshape(31, D)
    pvec[44:47, 0:512] = f(inputs["b_conv_w"]).reshape(3, 512)
    return {
        "wt32": host_weight_tiles(inputs),
        "pvec": pvec,
        "ws": f(inputs["a_spatial_w"]).reshape(8 * 128, 128),
        "bs": f(inputs["a_spatial_b"]).reshape(1, 8 * 128),
    }


def kernel(**inputs):
    xp = np.asarray(inputs["x_prompt"], dtype=np.float32)
    xs = np.asarray(inputs["x_sample"], dtype=np.float32)
    shared = make_host_inputs(inputs)
    nb = TOK_PER_CORE // T
    nc = build_nc(nb, seq_starts={0, 8, 16}, seq_ends={7, 15, 31})
    in_maps = []
    for i in range(NCORES):
        xc = np.concatenate([xp[2 * i].reshape(-1, D), xp[2 * i + 1].reshape(-1, D), xs[i].reshape(-1, D)], axis=0)
        m = dict(shared); m["x"] = np.ascontiguousarray(xc)
        in_maps.append(m)
    res = run_bass_kernel_spmd(nc, in_maps, core_ids=list(range(NCORES)))
    yp = np.empty_like(xp); ys = np.empty_like(xs)
    for i in range(NCORES):
        yc = res.results[i]["y"]
        yp[2 * i] = yc[0:4096]; yp[2 * i + 1] = yc[4096:8192]; ys[i] = yc[8192:16384]
    return (yp, ys)
```

```python
from contextlib import ExitStack
import numpy as np
import concourse.bass as bass
import concourse.mybir as mybir
from concourse.bass_utils import run_bass_kernel_spmd

F32 = mybir.dt.float32
BF16 = mybir.dt.bfloat16
AF = mybir.ActivationFunctionType
ALU = mybir.AluOpType

D = 1024
DFF = 2816
NFC = 22
T = 512
NCORES = 8
TOK_PER_CORE = 16384
RMS_EPS = 1e-6
LN_EPS = 1e-5
NW = 4
WCOLS = 4096
NPV = 47

PV_FFN = 0
PV_MIX = 4
PV_FINAL = 6
PV_B1V = 7
PV_B1G = 8
PV_DWB = 9
PV_LNG = 10
PV_LNB = 11
PV_B2 = 12
PV_DW = 13
PV_BC = 44

def GU(k, j): return k * 19 + j
def DN(k, o): return k * 19 + 11 + o
WIN_U = 76
WIN_V = 77
def PB(c): return 78 + c
def WOUT(t): return 82 + t
def PW1(t): return 84 + t
def PW2(t): return 88 + t
NTILES = 90
A_TILES = [GU(0, j) for j in range(11)] + [DN(0, o) for o in range(8)] + [PB(c) for c in range(4)] + [WIN_V, WIN_U]
B_TILES = ([WOUT(0), WOUT(1)] + [GU(1, j) for j in range(11)] + [DN(1, o) for o in range(8)]
           + [GU(2, j) for j in range(11)] + [DN(2, o) for o in range(8)] + [PW1(t) for t in range(4)])
C_TILES = [PW2(0), PW2(1)] + [GU(3, j) for j in range(11)] + [DN(3, o) for o in range(8)]


def tile_used(idx):
    if idx < 76:
        return 4096 if (idx % 19) < 11 else NFC * 128
    if idx in (WIN_U, WIN_V):
        return 4096
    if idx < 82:
        return 3072
    return 4096


def host_weight_tiles(inputs):
    f = lambda a: np.asarray(a, dtype=np.float32)
    wg = f(inputs["ffn_w_gate"]).reshape(4, D, DFF); wu = f(inputs["ffn_w_up"]).reshape(4, D, DFF)
    wd = f(inputs["ffn_w_down"]).reshape(4, DFF, D)
    win = f(inputs["ab_w_in"]).reshape(D, 2560); wout = f(inputs["ab_w_out"]).reshape(D, D)
    pw1 = f(inputs["c_w_pw1"]).reshape(D, 2 * D); pw2 = f(inputs["c_w_pw2"]).reshape(D, D)
    out = np.zeros((NTILES, 128, WCOLS), np.float32)

    def kmaj(m):
        K, n = m.shape
        return m.reshape(K // 128, 128, n).transpose(1, 0, 2).reshape(128, (K // 128) * n)

    for k in range(4):
        for j in range(11):
            out[GU(k, j), :, 0:2048] = kmaj(wg[k][:, j * 256:(j + 1) * 256])
            out[GU(k, j), :, 2048:4096] = kmaj(wu[k][:, j * 256:(j + 1) * 256])
        for o in range(8):
            out[DN(k, o), :, 0:NFC * 128] = kmaj(wd[k][:, o * 128:(o + 1) * 128])
    out[WIN_U] = kmaj(win[:, 0:512]); out[WIN_V] = kmaj(win[:, 512:1024])
    for c in range(4):
        for w in range(3):
            out[PB(c), :, w * 1024:(w + 1) * 1024] = kmaj(win[:, 1024 + w * 512 + c * 128: 1024 + w * 512 + (c + 1) * 128])
    for t in range(2):
        out[WOUT(t)] = kmaj(wout[:, t * 512:(t + 1) * 512])
        out[PW2(t)] = kmaj(pw2[:, t * 512:(t + 1) * 512])
    for t in range(4):
        out[PW1(t), :, 0:2048] = kmaj(pw1[:, t * 256:(t + 1) * 256])
        out[PW1(t), :, 2048:4096] = kmaj(pw1[:, D + t * 256: D + (t + 1) * 256])
    return out


class Res:
    __slots__ = ("w", "r")
    def __init__(self):
        self.w = None; self.r = {}


class Eng:
    def __init__(self, name, handle, sem, inorder=False):
        self.name = name; self.h = handle; self.sem = sem; self.count = 0
        self.seen = {}; self.inorder = inorder


class Chan:
    def __init__(self, sem):
        self.sem = sem; self.count = 0


class Sched:
    def __init__(self, nc, stack):
        self.nc = nc; self.stack = stack
        mk = lambda n: stack.enter_context(nc.semaphore(n))
        self.pe = Eng("pe", nc.tensor, mk("s_pe"), inorder=True)
        self.act = Eng("act", nc.scalar, mk("s_act"))
        self.dve = Eng("dve", nc.vector, mk("s_dve"))
        self.pool = Eng("pool", nc.gpsimd, mk("s_pool"))
        self.sp = Eng("sp", nc.sync, mk("s_sp"))
        self.engs = [self.pe, self.act, self.dve, self.pool, self.sp]
        self.chans = []
        self.fence_t = stack.enter_context(nc.sbuf_tensor("fence_t", [128, 8], F32))

    def _fence(self, eng, res_list):
        f = self.fence_t
        if eng is self.act:
            emit = lambda: self.nc.scalar.copy(out=f[:, 1:2], in_=f[:, 0:1])
        elif eng is self.dve:
            emit = lambda: self.nc.vector.tensor_copy(out=f[:, 3:4], in_=f[:, 2:3])
        else:
            emit = lambda: self.nc.gpsimd.tensor_copy(out=f[:, 5:6], in_=f[:, 4:5])
        self.op(eng, emit, reads=res_list, writes=res_list)

    def chan(self, name):
        c = Chan(self.stack.enter_context(self.nc.semaphore(name)))
        self.chans.append(c); return c

    def op(self, eng, emit, reads=(), writes=(), chan=None, inc=True):
        if chan is not None:
            byeng = {}
            for r in reads:
                tok = r.w
                if tok is not None and isinstance(tok[0], Eng) and tok[0] is not self.pe and not getattr(r, "fenced", False):
                    byeng.setdefault(tok[0], []).append(r)
            for e, rl in byeng.items():
                self._fence(e, rl)
        needs = {}
        for r in reads:
            tok = r.w
            if tok is not None and needs.get(tok[0], 0) < tok[1]:
                needs[tok[0]] = tok[1]
        for w in writes:
            tok = w.w
            if tok is not None and needs.get(tok[0], 0) < tok[1]:
                needs[tok[0]] = tok[1]
            for s, c in w.r.items():
                if needs.get(s, 0) < c:
                    needs[s] = c
        for s, c in needs.items():
            if s is eng and eng.inorder:
                continue
            if eng.seen.get(s, 0) >= c:
                continue
            eng.h.wait_ge(s.sem, c); eng.seen[s] = c
        ins = emit()
        if chan is not None:
            chan.count += 16; ins.then_inc(chan.sem, 16); tok = (chan, chan.count)
        elif inc:
            eng.count += 1; ins.then_inc(eng.sem, 1); tok = (eng, eng.count)
        else:
            tok = (eng, eng.count + 1)
        for r in reads:
            if r.r.get(tok[0], 0) < tok[1]:
                r.r[tok[0]] = tok[1]
        for w in writes:
            w.w = tok; w.r = {}
        return ins

    def barrier(self):
        srcs = [e for e in self.engs if e.count > 0] + [c for c in self.chans if c.count > 0]
        for e in self.engs:
            for s in srcs:
                if s is e:
                    continue
                if e.seen.get(s, 0) >= s.count:
                    continue
                e.h.wait_ge(s.sem, s.count); e.seen[s] = s.count


class Pool:
    def __init__(self, tiles):
        self.tiles = tiles; self.res = [Res() for _ in tiles]
        self.live = [False] * len(tiles); self.nxt = 0

    def alloc(self):
        n = len(self.tiles)
        for k in range(n):
            i = (self.nxt + k) % n
            if not self.live[i]:
                self.live[i] = True; self.nxt = (i + 1) % n
                return i
        raise RuntimeError("pool exhausted")

    def free(self, i):
        assert self.live[i]; self.live[i] = False


def build_nc(nb, seq_starts, seq_ends):
    ntok = nb * T
    nc = bass.Bass("TRN2", target_bir_lowering=False)
    dt = lambda name, shape, dtp=F32, kind="ExternalInput": nc.dram_tensor(name, shape, dtp, kind=kind).ap()
    x_d = dt("x", [ntok, D])
    y_d = dt("y", [ntok, D], kind="ExternalOutput")
    wt32_d = dt("wt32", [NTILES, 128, WCOLS])
    pv_d = dt("pvec", [NPV, D]); ws_d = dt("ws", [8 * 128, 128]); bs_d = dt("bs", [1, 8 * 128])
    wsc = dt("wsc", [NTILES, 128, WCOLS], BF16, kind="Internal")

    with ExitStack() as st:
        S = Sched(nc, st)
        op = S.op
        PE, ACT, DVE, POOL, SP = S.pe, S.act, S.dve, S.pool, S.sp
        sb = lambda n, s, d: st.enter_context(nc.sbuf_tensor(n, s, d))
        ident = sb("ident", [128, 128], F32); R_ident = Res()
        ones = sb("ones", [128, 128], BF16); R_ones = Res()
        pv = sb("pv", [128, 8, 48], F32); R_pv = Res()
        wsT = sb("wsT", [128, 8, 128], BF16); R_wsT = Res()
        bhi = sb("bhi", [1, 1024], BF16); blo = sb("blo", [1, 1024], BF16); R_bias = Res()
        b2 = sb("b2", [2, 1024], BF16); R_b2 = Res()
        banks = [st.enter_context(nc.psum_tensor(f"bank{i}", [128, T], F32)) for i in range(8)]
        PS = Pool(banks)
        st.enter_context(nc.Block())
        R_fence = Res()
        op(DVE, lambda: nc.vector.memset(S.fence_t[:], 0.0), writes=[R_fence])
        op(ACT, lambda: nc.scalar.copy(out=S.fence_t[:, 1:2], in_=S.fence_t[:, 0:1]), reads=[R_fence], writes=[R_fence])
        op(POOL, lambda: nc.gpsimd.tensor_copy(out=S.fence_t[:, 5:6], in_=S.fence_t[:, 4:5]), reads=[R_fence], writes=[R_fence])
        c_misc = [S.chan("c_misc0"), S.chan("c_misc1"), S.chan("c_misc2")]
        R_wsc = {idx: Res() for idx in range(NTILES)}

        with ExitStack() as pst:
            psb = lambda n, s, d: pst.enter_context(nc.sbuf_tensor(n, s, d))
            iot = psb("iot", [128, 128], F32); R_iot = Res()
            pstg = psb("pstg", [NPV, D], F32); R_pstg = Res()
            wsl = psb("wsl", [128, 8, 128], F32); R_wsl = Res()
            bsf = psb("bsf", [1, 1024], F32); bsf2 = psb("bsf2", [1, 1024], F32); R_bsf = Res()
            op(POOL, lambda: nc.gpsimd.iota(iot[:], pattern=[[1, 128]], base=0, channel_multiplier=-1,
                                            allow_small_or_imprecise_dtypes=True), writes=[R_iot])
            op(DVE, lambda: nc.vector.tensor_single_scalar(out=ident[:], in_=iot[:], scalar=0.0, op=ALU.is_equal),
               reads=[R_iot], writes=[R_ident])
            op(DVE, lambda: nc.vector.memset(ones[:], 1.0), writes=[R_ones])
            op(SP, lambda: nc.sync.dma_start(out=pstg[:], in_=pv_d[:, :]), writes=[R_pstg], chan=c_misc[0])
            op(SP, lambda: nc.sync.dma_start(out=wsl[:], in_=ws_d.rearrange("(h p) q -> p h q", p=128)),
               writes=[R_wsl], chan=c_misc[1])
            op(SP, lambda: nc.sync.dma_start(out=bsf[:], in_=bs_d[:, :]), writes=[R_bsf], chan=c_misc[2])
            for c in range(8):
                b = PS.alloc()
                op(PE, lambda b=b, c=c: nc.tensor.transpose(banks[b][:, 0:NPV], pstg[0:NPV, c * 128:(c + 1) * 128],
                                                            ident[0:NPV, 0:NPV]),
                   reads=[R_pstg, R_ident], writes=[PS.res[b]])
                op(ACT, lambda b=b, c=c: nc.scalar.copy(out=pv[:, c, 0:NPV], in_=banks[b][:, 0:NPV]),
                   reads=[PS.res[b]], writes=[R_pv])
                PS.free(b)
            for h in range(8):
                b = PS.alloc()
                op(PE, lambda b=b, h=h: nc.tensor.transpose(banks[b][:, 0:128], wsl[:, h, :], ident[:]),
                   reads=[R_wsl, R_ident], writes=[PS.res[b]])
                op(ACT, lambda b=b, h=h: nc.scalar.copy(out=wsT[:, h, :], in_=banks[b][:, 0:128]),
                   reads=[PS.res[b]], writes=[R_wsT])
                PS.free(b)
            op(DVE, lambda: nc.vector.tensor_copy(out=bhi[:], in_=bsf[:]), reads=[R_bsf], writes=[R_bias])
            op(DVE, lambda: nc.vector.tensor_copy(out=bsf2[:], in_=bhi[:]), reads=[R_bias], writes=[R_bsf])
            op(DVE, lambda: nc.vector.tensor_tensor(out=blo[:], in0=bsf[:], in1=bsf2[:], op=ALU.subtract),
               reads=[R_bsf], writes=[R_bias])
            c_b2 = [S.chan("c_b2a"), S.chan("c_b2b")]
            op(SP, lambda: nc.sync.dma_start(out=b2[0:1, :], in_=bhi[0:1, :]), reads=[R_bias], writes=[R_b2], chan=c_b2[0])
            op(SP, lambda: nc.sync.dma_start(out=b2[1:2, :], in_=blo[0:1, :]), reads=[R_bias], writes=[R_b2], chan=c_b2[1])

            S.barrier()

        X = [sb(f"X{i}", [128, 8, T], F32) for i in range(3)]
        RX = [[Res() for _ in range(8)] for _ in range(3)]
        c_xin = [S.chan(f"c_xin{i}") for i in range(4)]
        c_xin_e = [[S.chan(f"c_xine{i}{j}") for j in range(2)] for i in range(2)]
        XOUTT = sb("XOUT", [128, 2, D], F32); XOUT = [XOUTT[:, 0, :], XOUTT[:, 1, :]]; R_XOUT = [Res(), Res()]
        c_xout = [S.chan("c_xout0"), S.chan("c_xout1")]
        H = sb("H", [128, 8, T], BF16); RH = [Res() for _ in range(8)]
        Y = [sb(f"Y{i}", [128, 8, T], BF16) for i in range(2)]; RY = [[Res() for _ in range(8)] for _ in range(2)]
        A = sb("A", [128, NFC, T], BF16); RA = [Res() for _ in range(NFC)]
        AXf = A[:, 0:16, :].rearrange("p a b -> p (a b)").bitcast(F32)
        W = [sb(f"W{i}", [128, WCOLS], BF16) for i in range(NW)]; RW = [Res() for _ in range(NW)]
        c_w = [S.chan(f"c_w{i}") for i in range(NW)]
        Z = [sb(f"Z{i}", [128, 8, T + 30], BF16) for i in range(2)]; RZ = [[Res() for _ in range(8)] for _ in range(2)]
        RZH = [[Res() for _ in range(8)] for _ in range(2)]
        HS = sb("HS", [128, 8, 15], F32); R_HS = [Res() for _ in range(8)]
        C = sb("C", [128, 8, T], F32); RC = [Res() for _ in range(8)]
        V = A; RV = RA
        TP = Pool([sb(f"TP{i}", [128, T], F32) for i in range(8)])
        SQ = Pool([sb(f"SQ{i}", [128, T], BF16) for i in range(8)])
        CX = [sb(f"CX{i}", [128, T + 2], F32) for i in range(2)]; R_CX = [Res(), Res()]
        CXP = sb("CXP", [128, 4], F32); R_CXP = [Res() for _ in range(4)]
        CXH = sb("CXH", [128, 4], F32); R_CXH = Res()
        PL = [sb(f"PL{i}", [128, 4], F32) for i in range(2)]; R_PL = [Res(), Res()]
        BL = [sb(f"BL{i}", [128, 4], F32) for i in range(2)]; R_BL = [Res(), Res()]
        FX = sb("FX", [128, 4], F32); R_FX = Res()
        WARM = sb("WARM", [128, 2], F32); R_WARM = Res(); R_WARM2 = Res()
        op(DVE, lambda: nc.vector.memset(WARM[:], 1.0), writes=[R_WARM])

        for i in range(2):
            op(DVE, lambda i=i: nc.vector.memset(Z[i][:, :, T + 15:T + 30], 0.0), writes=RZ[i])
            op(DVE, lambda i=i: nc.vector.memset(CX[i][:, T + 1:T + 2], 0.0), writes=[R_CX[i]])

        assert nb >= 3
        cv_order = [(i, 'D') for i in A_TILES]
        _cq = list(C_TILES)
        for _n, _i in enumerate(B_TILES):
            cv_order.append((_i, 'D'))
            if _n % 2 == 1 and _cq:
                cv_order.append((_cq.pop(0), 'I'))
        cv_order += [(_i, 'I') for _i in _cq]
        cv_pos = {it[0]: n for n, it in enumerate(cv_order)}
        stg32 = [X[2][:].rearrange("p a b -> p (a b)"), C[:].rearrange("p a b -> p (a b)")]
        stgR = [RX[2], RC]
        stb16 = [Z[1][:].rearrange("p a b -> p (a b)")[:, 0:WCOLS], XOUTT[:].rearrange("p a b -> p (a b)").bitcast(BF16)]
        stbR = [RZ[1] + RZH[1], R_XOUT]
        c_cvl = [S.chan("c_cvl0"), S.chan("c_cvl1")]
        c_cvs = [S.chan("c_cvs0"), S.chan("c_cvs1")]
        c_cvd = [S.chan(f"c_cvd{i}") for i in range(NW)]
        cv = {"loaded": 0, "done": 0, "tick": 0, "ni": 0}

        def cv_load(n):
            idx = cv_order[n][0]; used = tile_used(idx); k = n % 2
            op(POOL, lambda: nc.gpsimd.dma_start(out=stg32[k][:, 0:used], in_=wt32_d[idx, :, 0:used]), writes=stgR[k], chan=c_cvl[k])

        def cv_prefetch():
            while cv["loaded"] < min(cv["done"] + 2, len(cv_order)):
                cv_load(cv["loaded"]); cv["loaded"] += 1

        def cv_indirect():
            n = cv["done"]; idx, mode = cv_order[n]
            assert mode == 'I'
            cv_prefetch()
            used = tile_used(idx); k = n % 2; j = cv["ni"] % 2; cv["ni"] += 1
            op(DVE, lambda: nc.vector.tensor_copy(out=stb16[j][:, 0:used], in_=stg32[k][:, 0:used]), reads=stgR[k], writes=stbR[j])
            op(POOL, lambda: nc.gpsimd.dma_start(out=wsc[idx, :, 0:used], in_=stb16[j][:, 0:used]), reads=stbR[j],
               writes=[R_wsc[idx]], chan=c_cvs[j])
            cv["done"] += 1
            cv_prefetch()

        def cv_direct(idx, slot):
            while cv_order[cv["done"]][0] != idx:
                cv_indirect()
            n = cv["done"]
            assert cv_order[n] == (idx, 'D')
            cv_prefetch()
            used = tile_used(idx); k = n % 2
            op(DVE, lambda: nc.vector.tensor_copy(out=W[slot][:, 0:used], in_=stg32[k][:, 0:used]), reads=stgR[k], writes=[RW[slot]])
            op(POOL, lambda: nc.gpsimd.dma_start(out=wsc[idx, :, 0:used], in_=W[slot][:, 0:used]), reads=[RW[slot]],
               writes=[R_wsc[idx]], chan=c_cvd[slot])
            cv["done"] += 1
            cv_prefetch()

        def cv_background():
            if cv["done"] < len(cv_order) and cv_order[cv["done"]][1] == 'I':
                cv_indirect()

        def cv_ensure(idx):
            pos = cv_pos[idx]
            while cv["done"] <= pos:
                cv_indirect()

        def cv_finish():
            while cv["done"] < len(cv_order):
                cv_indirect()
            op(DVE, lambda: nc.vector.memset(Z[1][:, :, T + 15:T + 30], 0.0), writes=RZ[1])

        A_T = A_TILES
        B_T = B_TILES
        C_T = C_TILES
        plan = []
        for s in range(nb + 2):
            if s < nb: plan += A_T
            if 0 <= s - 1 < nb: plan += B_T
            if 0 <= s - 2 < nb: plan += C_T
        TILE_USED = {idx: tile_used(idx) for idx in range(NTILES)}
        wstate = {"issued": 0, "used": 0}

        def w_issue_upto(n):
            while wstate["issued"] < min(n, len(plan)):
                i = wstate["issued"]; slot = i % NW; idx = plan[i]; used = TILE_USED[idx]
                pos = cv_pos[idx]
                if pos >= cv["done"] and cv_order[pos][1] == 'D':
                    cv_direct(idx, slot)
                else:
                    cv_ensure(idx)
                    op(SP, lambda slot=slot, idx=idx, used=used: nc.sync.dma_start(out=W[slot][:, 0:used], in_=wsc[idx, :, 0:used]),
                       reads=[R_wsc[idx]], writes=[RW[slot]], chan=c_w[slot])
                wstate["issued"] += 1

        def wget(idx):
            i = wstate["used"]
            assert plan[i] == idx, (i, plan[i], idx)
            w_issue_upto(i + NW - 1)
            wstate["used"] += 1
            slot = i % NW
            return W[slot], RW[slot]

        bg = []

        def pump(n):
            cv["tick"] += 1
            if cv["tick"] % 3 == 0:
                cv_background()
            for _ in range(n):
                if not bg:
                    return
                bg.pop(0)()

        def drain():
            while bg:
                bg.pop(0)()

        def mm_group(bank, lhs_list, rhs_list, reads, per=None):
            n = len(lhs_list)
            o = banks[bank][:]
            for i in range(n):
                rd = reads if per is None else reads + [per[i]]
                op(PE, lambda i=i: nc.tensor.matmul(o, lhs_list[i], rhs_list[i], start=(i == 0), stop=(i == n - 1)),
                   reads=rd, writes=[PS.res[bank]], inc=(i == n - 1))

        def mm_multi(groups):
            for i in range(8):
                for (bank, lhs_list, rhs_list, reads, per) in groups:
                    op(PE, lambda: nc.tensor.matmul(banks[bank][:], lhs_list[i], rhs_list[i], start=(i == 0), stop=(i == 7)),
                       reads=reads + [per[i]], writes=[PS.res[bank]], inc=(i == 7))

        def rstd_from_bank(sbank, eps):
            r = TP.alloc()
            op(ACT, lambda: nc.scalar.activation(out=TP.tiles[r][:], in_=banks[sbank][:], func=AF.Ln, bias=eps, scale=1.0 / D),
               reads=[PS.res[sbank]], writes=[TP.res[r]])
            PS.free(sbank)
            op(ACT, lambda: nc.scalar.activation(out=TP.tiles[r][:], in_=TP.tiles[r][:], func=AF.Exp, scale=-0.5),
               reads=[TP.res[r]], writes=[TP.res[r]])
            return r

        def warm_sqrt():
            op(ACT, lambda: nc.scalar.activation(out=WARM[:, 1:2], in_=WARM[:, 0:1], func=AF.Ln), reads=[R_WARM], writes=[R_WARM2])

        def rms_to_H(xb, rx, grow):
            warm_sqrt()
            sbank = PS.alloc()
            for c in range(8):
                q = SQ.alloc()
                op(ACT, lambda c=c, q=q: nc.scalar.activation(out=SQ.tiles[q][:], in_=xb[:, c, :], func=AF.Square),
                   reads=[rx[c]], writes=[SQ.res[q]])
                op(PE, lambda c=c, q=q: nc.tensor.matmul(banks[sbank][:], ones[:], SQ.tiles[q][:], start=(c == 0), stop=(c == 7)),
                   reads=[R_ones, SQ.res[q]], writes=[PS.res[sbank]], inc=True)
                SQ.free(q)
            r = rstd_from_bank(sbank, RMS_EPS)
            for c in range(8):
                op(DVE, lambda c=c: nc.vector.scalar_tensor_tensor(out=H[:, c, :], in0=xb[:, c, :], scalar=pv[:, c, grow:grow + 1],
                                                                   in1=TP.tiles[r][:], op0=ALU.mult, op1=ALU.mult),
                   reads=[rx[c], TP.res[r], R_pv], writes=[RH[c]])
            TP.free(r)

        def ffn(k, xb, rx, skip_norm=False):
            if not skip_norm:
                rms_to_H(xb, rx, PV_FFN + k)
            def evac_pair(f, bg_, bu_):
                sg = TP.alloc()
                op(ACT, lambda: nc.scalar.activation(out=TP.tiles[sg][:], in_=banks[bg_][:], func=AF.Silu),
                   reads=[PS.res[bg_]], writes=[TP.res[sg]])
                op(DVE, lambda: nc.vector.tensor_tensor(out=A[:, f, :], in0=TP.tiles[sg][:], in1=banks[bu_][:], op=ALU.mult),
                   reads=[TP.res[sg], PS.res[bu_]], writes=[RA[f]])
                TP.free(sg); PS.free(bg_); PS.free(bu_)

            hk = [H[:, kc, :] for kc in range(8)]
            w0, rw0 = wget(GU(k, 0)); w1, rw1 = wget(GU(k, 1))
            groups = []; pairs = []
            for (wt, rw, j) in ((w0, rw0, 0), (w1, rw1, 1)):
                for fp in range(2):
                    bg_ = PS.alloc(); bu_ = PS.alloc()
                    groups.append((bg_, [wt[:, kc * 256 + fp * 128: kc * 256 + fp * 128 + 128] for kc in range(8)], hk, [rw], RH))
                    groups.append((bu_, [wt[:, 2048 + kc * 256 + fp * 128: 2048 + kc * 256 + fp * 128 + 128] for kc in range(8)], hk, [rw], RH))
                    pairs.append((2 * j + fp, bg_, bu_))
            mm_multi(groups)
            for (f, bg_, bu_) in pairs:
                evac_pair(f, bg_, bu_)
                pump(3)
            for j in range(2, 11):
                wt, rw = wget(GU(k, j))
                for fp in range(2):
                    f = 2 * j + fp
                    bg_ = PS.alloc(); bu_ = PS.alloc()
                    mm_group(bg_, [wt[:, kc * 256 + fp * 128: kc * 256 + fp * 128 + 128] for kc in range(8)], hk, reads=[rw], per=RH)
                    mm_group(bu_, [wt[:, 2048 + kc * 256 + fp * 128: 2048 + kc * 256 + fp * 128 + 128] for kc in range(8)], hk, reads=[rw], per=RH)
                    evac_pair(f, bg_, bu_)
                    pump(3)
            for o in range(8):
                wt, rw = wget(DN(k, o))
                bo = PS.alloc()
                mm_group(bo, [wt[:, fc * 128:(fc + 1) * 128] for fc in range(NFC)], [A[:, fc, :] for fc in range(NFC)],
                         reads=[rw], per=RA)
                op(DVE, lambda: nc.vector.scalar_tensor_tensor(out=xb[:, o, :], in0=banks[bo][:], scalar=0.5, in1=xb[:, o, :],
                                                               op0=ALU.mult, op1=ALU.add),
                   reads=[PS.res[bo], rx[o]], writes=[rx[o]])
                PS.free(bo)
                if o < 6:
                    pump(7)

        def x_src(b, s4):
            if s4 < 2:
                v = Y[b % 2][:, 4 * s4:4 * s4 + 4, :].rearrange("p a b -> p (a b)").bitcast(F32)
                return v, RY[b % 2][4 * s4:4 * s4 + 4]
            k = s4 - 2
            return AXf[:, k * 1024:(k + 1) * 1024], RA[4 * k:4 * k + 4]

        def load_x_early(b):
            for s4 in range(2):
                v, rr = x_src(b, s4)
                r0 = b * T + s4 * 128
                op(POOL, lambda: nc.gpsimd.dma_start(out=v, in_=x_d[r0:r0 + 128, :]), writes=rr, chan=c_xin_e[b % 2][s4])

        def load_x_late(b):
            for s4 in range(2, 4):
                v, rr = x_src(b, s4)
                r0 = b * T + s4 * 128
                op(ACT, lambda: nc.scalar.dma_start(out=v, in_=x_d[r0:r0 + 128, :]), writes=rr, chan=c_xin[s4])

        def stage_A_head(b):
            xb = X[b % 3]; rx = RX[b % 3]
            for s4 in range(4):
                v, rr = x_src(b, s4)
                for half in range(2):
                    bk = PS.alloc()
                    for cc in range(4):
                        c = half * 4 + cc
                        op(PE, lambda: nc.tensor.transpose(banks[bk][:, cc * 128:(cc + 1) * 128], v[:, c * 128:(c + 1) * 128], ident[:]),
                           reads=rr + [R_ident], writes=[PS.res[bk]], inc=(cc == 3))
                    o_ap = xb[:, half * 4:half * 4 + 4, s4 * 128:(s4 + 1) * 128]
                    i_ap = banks[bk][:].rearrange("p (a b) -> p a b", b=128)
                    if half == 0:
                        op(ACT, lambda: nc.scalar.copy(out=o_ap, in_=i_ap), reads=[PS.res[bk]], writes=rx[half * 4:half * 4 + 4])
                    else:
                        op(DVE, lambda: nc.vector.tensor_copy(out=o_ap, in_=i_ap), reads=[PS.res[bk]], writes=rx[half * 4:half * 4 + 4])
                    PS.free(bk)
            rms_to_H(xb, rx, PV_FFN + 0)

        def stage_A(b):
            xb = X[b % 3]; rx = RX[b % 3]; yb = Y[b % 2]; ry = RY[b % 2]
            ffn(0, xb, rx, skip_norm=True)
            rms_to_H(xb, rx, PV_MIX + 0)
            for c in range(4):
                wt, rw = wget(PB(c))
                bb = PS.alloc(); bc = PS.alloc(); bx = PS.alloc()
                for (bk, wsel) in ((bc, 1), (bx, 2), (bb, 0)):
                    mm_group(bk, [wt[:, wsel * 1024 + kc * 128: wsel * 1024 + (kc + 1) * 128] for kc in range(8)],
                             [H[:, kc, :] for kc in range(8)], reads=[rw], per=RH)
                xs = TP.alloc()
                op(ACT, lambda: nc.scalar.copy(out=TP.tiles[xs][:], in_=banks[bx][:]), reads=[PS.res[bx]], writes=[TP.res[xs]])
                PS.free(bx)
                cx = CX[c % 2]; rcx = R_CX[c % 2]
                op(DVE, lambda: nc.vector.tensor_tensor(out=cx[:, 1:T + 1], in0=banks[bc][:], in1=TP.tiles[xs][:], op=ALU.mult),
                   reads=[PS.res[bc], TP.res[xs]], writes=[rcx])
                TP.free(xs); PS.free(bc)
                if b in seq_starts:
                    op(DVE, lambda: nc.vector.memset(cx[:, 0:1], 0.0), writes=[rcx])
                else:
                    op(DVE, lambda: nc.vector.tensor_copy(out=cx[:, 0:1], in_=CXP[:, c:c + 1]), reads=[R_CXP[c]], writes=[rcx])
                t1 = TP.alloc(); tt = TP.tiles[t1]
                op(DVE, lambda: nc.vector.tensor_scalar_mul(out=tt[:], in0=cx[:, 0:T], scalar1=pv[:, c, PV_BC:PV_BC + 1]),
                   reads=[rcx, R_pv], writes=[TP.res[t1]])
                op(DVE, lambda: nc.vector.scalar_tensor_tensor(out=tt[:], in0=cx[:, 1:T + 1], scalar=pv[:, c, PV_BC + 1:PV_BC + 2], in1=tt[:],
                                                               op0=ALU.mult, op1=ALU.add),
                   reads=[rcx, R_pv, TP.res[t1]], writes=[TP.res[t1]])
                op(DVE, lambda: nc.vector.scalar_tensor_tensor(out=tt[:], in0=cx[:, 2:T + 2], scalar=pv[:, c, PV_BC + 2:PV_BC + 3], in1=tt[:],
                                                               op0=ALU.mult, op1=ALU.add),
                   reads=[rcx, R_pv, TP.res[t1]], writes=[TP.res[t1]])
                op(DVE, lambda: nc.vector.tensor_copy(out=PL[b % 2][:, c:c + 1], in_=tt[:, T - 1:T]), reads=[TP.res[t1]], writes=[R_PL[b % 2]])
                op(DVE, lambda: nc.vector.tensor_copy(out=BL[b % 2][:, c:c + 1], in_=banks[bb][:, T - 1:T]), reads=[PS.res[bb]], writes=[R_BL[b % 2]])
                op(DVE, lambda: nc.vector.tensor_copy(out=CXP[:, c:c + 1], in_=cx[:, T:T + 1]), reads=[rcx], writes=[R_CXP[c]])
                op(DVE, lambda: nc.vector.tensor_copy(out=CXH[:, c:c + 1], in_=cx[:, 1:2]), reads=[rcx], writes=[R_CXH])
                op(DVE, lambda: nc.vector.tensor_tensor(out=yb[:, 4 + c, :], in0=tt[:], in1=banks[bb][:], op=ALU.mult),
                   reads=[TP.res[t1], PS.res[bb]], writes=[ry[4 + c]])
                TP.free(t1); PS.free(bb)
            wv, rwv = wget(WIN_V)
            wu, rwu = wget(WIN_U)
            vb = [PS.alloc() for _ in range(4)]
            mm_multi([(vb[s4], [H[:, kc, s4 * 128:(s4 + 1) * 128] for kc in range(8)], [wv[:, kc * 512:(kc + 1) * 512] for kc in range(8)],
                       [rwv], RH) for s4 in range(4)])
            for s4 in range(4):
                bk = vb[s4]
                op(ACT, lambda: nc.scalar.activation(out=V[:, s4, :], in_=banks[bk][:], func=AF.Gelu),
                   reads=[PS.res[bk]], writes=[RV[s4]])
                PS.free(bk)
            mb = [PS.alloc() for _ in range(4)]
            for s4 in range(4):
                for hc in range(4):
                    for hh in range(2):
                        h = 2 * hc + hh
                        o = banks[mb[hc]][hh * 64:(hh + 1) * 64, s4 * 128:(s4 + 1) * 128]
                        op(PE, lambda: nc.tensor.matmul(o, ones[0:2, 0:64], b2[0:2, h * 128:(h + 1) * 128], start=(s4 == 0), stop=False),
                           reads=[R_ones, R_b2], writes=[PS.res[mb[hc]]], inc=False)
            ugs = []
            for c in range(4):
                bk = PS.alloc()
                mm_group(bk, [wu[:, kc * 512 + c * 128: kc * 512 + (c + 1) * 128] for kc in range(8)], [H[:, kc, :] for kc in range(8)],
                         reads=[rwu], per=RH)
                ug = TP.alloc()
                op(ACT, lambda: nc.scalar.activation(out=TP.tiles[ug][:], in_=banks[bk][:], func=AF.Gelu),
                   reads=[PS.res[bk]], writes=[TP.res[ug]])
                PS.free(bk)
                ugs.append(ug)
            for s4 in range(4):
                for hc in range(4):
                    for hh in range(2):
                        h = 2 * hc + hh
                        o = banks[mb[hc]][hh * 64:(hh + 1) * 64, s4 * 128:(s4 + 1) * 128]
                        op(PE, lambda: nc.tensor.matmul(o, V[:, s4, h * 64:(h + 1) * 64], wsT[:, h, :], start=False, stop=(s4 == 3)),
                           reads=[RV[s4], R_wsT], writes=[PS.res[mb[hc]]], inc=(s4 == 3 and hh == 1))
            for c in range(4):
                ug = ugs[c]
                op(DVE, lambda: nc.vector.tensor_tensor(out=yb[:, c, :], in0=TP.tiles[ug][:], in1=banks[mb[c]][:], op=ALU.mult),
                   reads=[TP.res[ug], PS.res[mb[c]]], writes=[ry[c]])
                TP.free(ug); PS.free(mb[c])

        def stage_B(b):
            xb = X[b % 3]; rx = RX[b % 3]; yb = Y[b % 2]; ry = RY[b % 2]
            zb = Z[b % 2]; rz = RZ[b % 2]
            if b not in seq_ends:
                op(DVE, lambda: nc.vector.tensor_tensor(out=FX[:], in0=CXH[:], in1=pv[:, 0:4, PV_BC + 2], op=ALU.mult),
                   reads=[R_CXH, R_pv], writes=[R_FX])
                op(DVE, lambda: nc.vector.tensor_tensor(out=FX[:], in0=FX[:], in1=PL[b % 2][:], op=ALU.add),
                   reads=[R_FX, R_PL[b % 2]], writes=[R_FX])
                op(DVE, lambda: nc.vector.tensor_tensor(out=yb[:, 4:8, T - 1], in0=FX[:], in1=BL[b % 2][:], op=ALU.mult),
                   reads=[R_FX, R_BL[b % 2]], writes=ry[4:8])
            ykc = [yb[:, kc, :] for kc in range(8)]
            for t in range(2):
                wt, rw = wget(WOUT(t))
                bks = [PS.alloc() for _ in range(4)]
                grp = [(bks[oc], [wt[:, kc * 512 + oc * 128: kc * 512 + (oc + 1) * 128] for kc in range(8)], ykc, [rw], ry) for oc in range(4)]
                if t == 0:
                    mm_multi(grp)
                else:
                    for g in grp:
                        mm_group(g[0], g[1], g[2], reads=g[3], per=g[4])
                for oc in range(4):
                    o = 4 * t + oc; bk = bks[oc]
                    op(DVE, lambda: nc.vector.tensor_tensor(out=xb[:, o, :], in0=banks[bk][:], in1=xb[:, o, :], op=ALU.add),
                       reads=[PS.res[bk], rx[o]], writes=[rx[o]])
                    PS.free(bk)
            if b + 2 < nb:
                load_x_early(b + 2)
            ffn(1, xb, rx)
            ffn(2, xb, rx)
            rms_to_H(xb, rx, PV_MIX + 1)
            bc = b - 1
            has_prev = bc >= 0
            do_tail = has_prev and (bc not in seq_ends)
            rzh = RZH[b % 2]
            if has_prev:
                drain()
                ln_banks[bc] = (PS.alloc(), PS.alloc())
            hb = [PS.alloc(), PS.alloc()]
            hk = [H[:, kc, :] for kc in range(8)]
            hk15 = [H[:, kc, 0:15] for kc in range(8)]
            tiles = {}

            def heads(t):
                wt, rw = tiles[t]
                bank = hb[t % 2]
                for cc in range(2):
                    for (vg, off) in ((0, 0), (1, 256)):
                        o = banks[bank][:, off + cc * 16: off + cc * 16 + 15]
                        for kc in range(8):
                            lw = wt[:, vg * 2048 + kc * 256 + cc * 128: vg * 2048 + kc * 256 + (cc + 1) * 128]
                            op(PE, lambda: nc.tensor.matmul(o, lw, hk15[kc], start=(kc == 0), stop=(kc == 7)),
                               reads=[rw, RH[kc]], writes=[PS.res[bank]], inc=(kc == 7))
                for cc in range(2):
                    c = 2 * t + cc
                    op(ACT, lambda: nc.scalar.activation(out=HS[:, c, :], in_=banks[bank][:, 256 + cc * 16: 256 + cc * 16 + 15],
                                                         func=AF.Sigmoid, bias=pv[:, c, PV_B1G:PV_B1G + 1]),
                       reads=[PS.res[bank], R_pv], writes=[R_HS[c]])
                    op(DVE, lambda: nc.vector.scalar_tensor_tensor(out=zb[:, c, 15:30], in0=banks[bank][:, cc * 16: cc * 16 + 15],
                                                                   scalar=pv[:, c, PV_B1V:PV_B1V + 1], in1=HS[:, c, :],
                                                                   op0=ALU.add, op1=ALU.mult),
                       reads=[PS.res[bank], R_HS[c], R_pv], writes=[rzh[c]])

            def tails(t):
                if not do_tail:
                    return
                for j in range(16, 31):
                    n = j - 15
                    for c in (2 * t, 2 * t + 1):
                        op(DVE, lambda: nc.vector.scalar_tensor_tensor(
                            out=C[:, c, T - n:T], in0=zb[:, c, 15:15 + n], scalar=pv[:, c, PV_DW + j:PV_DW + j + 1], in1=C[:, c, T - n:T],
                            op0=ALU.mult, op1=ALU.add), reads=[rzh[c], R_pv, RC[c]], writes=[RC[c]])

            tiles[0] = wget(PW1(0)); heads(0); tails(0)
            for t in range(4):
                if t < 3:
                    tiles[t + 1] = wget(PW1(t + 1)); heads(t + 1)
                wt, rw = tiles[t]
                for cc in range(2):
                    c = 2 * t + cc
                    bv = PS.alloc(); bgt = PS.alloc()
                    mm_group(bv, [wt[:, kc * 256 + cc * 128: kc * 256 + (cc + 1) * 128] for kc in range(8)], hk, reads=[rw], per=RH)
                    mm_group(bgt, [wt[:, 2048 + kc * 256 + cc * 128: 2048 + kc * 256 + (cc + 1) * 128] for kc in range(8)], hk,
                             reads=[rw], per=RH)
                    sg = TP.alloc()
                    op(ACT, lambda: nc.scalar.activation(out=TP.tiles[sg][:], in_=banks[bgt][:], func=AF.Sigmoid, bias=pv[:, c, PV_B1G:PV_B1G + 1]),
                       reads=[PS.res[bgt], R_pv], writes=[TP.res[sg]])
                    op(DVE, lambda: nc.vector.scalar_tensor_tensor(out=zb[:, c, 30:15 + T], in0=banks[bv][:, 15:T], scalar=pv[:, c, PV_B1V:PV_B1V + 1],
                                                                   in1=TP.tiles[sg][:, 15:T], op0=ALU.add, op1=ALU.mult),
                       reads=[PS.res[bv], TP.res[sg], R_pv], writes=[rz[c]])
                    TP.free(sg); PS.free(bv); PS.free(bgt)
                if t < 3:
                    tails(t + 1)
                ln_flush(2)
                if has_prev:
                    for cc in range(2):
                        ln_prep_chunk(bc, 2 * t + cc, None)
            ln_flush(0)
            PS.free(hb[0]); PS.free(hb[1])
            if has_prev:
                warm_sqrt()
            if b in seq_starts:
                op(DVE, lambda: nc.vector.memset(zb[:, :, 0:15], 0.0), writes=rz)
            else:
                zp = Z[(b - 1) % 2]
                op(DVE, lambda: nc.vector.tensor_copy(out=zb[:, :, 0:15], in_=zp[:, :, T:T + 15]), reads=RZ[(b - 1) % 2], writes=rz)

        def enqueue_conv(b):
            zb = Z[b % 2]; rz = RZ[b % 2]; rzh = RZH[b % 2]
            for cp in range(4):
                for j in range(31):
                    for c in (2 * cp, 2 * cp + 1):
                        if j == 0:
                            bg.append(lambda c=c: op(DVE, lambda: nc.vector.tensor_scalar(
                                out=C[:, c, :], in0=zb[:, c, 0:T], scalar1=pv[:, c, PV_DW:PV_DW + 1], scalar2=pv[:, c, PV_DWB:PV_DWB + 1],
                                op0=ALU.mult, op1=ALU.add), reads=[rz[c], rzh[c], R_pv], writes=[RC[c]]))
                        else:
                            bg.append(lambda c=c, j=j: op(DVE, lambda: nc.vector.scalar_tensor_tensor(
                                out=C[:, c, :], in0=zb[:, c, j:j + T], scalar=pv[:, c, PV_DW + j:PV_DW + j + 1], in1=C[:, c, :],
                                op0=ALU.mult, op1=ALU.add), reads=[rz[c], rzh[c], R_pv, RC[c]], writes=[RC[c]]))

        ln_banks = {}

        def ln_prep_chunk(bc, c, tail_from):
            if tail_from is not None:
                zn, rzn = tail_from
                for j in range(16, 31):
                    n = j - 15
                    op(DVE, lambda j=j, n=n: nc.vector.scalar_tensor_tensor(
                        out=C[:, c, T - n:T], in0=zn[:, c, 15:15 + n], scalar=pv[:, c, PV_DW + j:PV_DW + j + 1], in1=C[:, c, T - n:T],
                        op0=ALU.mult, op1=ALU.add), reads=[rzn[c], R_pv, RC[c]], writes=[RC[c]])
            q1 = SQ.alloc()
            op(ACT, lambda: nc.scalar.copy(out=SQ.tiles[q1][:], in_=C[:, c, :]), reads=[RC[c]], writes=[SQ.res[q1]])
            q2 = SQ.alloc()
            op(ACT, lambda: nc.scalar.activation(out=SQ.tiles[q2][:], in_=C[:, c, :], func=AF.Square), reads=[RC[c]], writes=[SQ.res[q2]])
            ln_pending.append((bc, c, q1, q2))

        ln_pending = []

        def ln_flush(keep):
            while len(ln_pending) > keep:
                bc, c, q1, q2 = ln_pending.pop(0)
                s1, s2 = ln_banks[bc]
                op(PE, lambda: nc.tensor.matmul(banks[s1][:], ones[:], SQ.tiles[q1][:], start=(c == 0), stop=(c == 7)),
                   reads=[R_ones, SQ.res[q1]], writes=[PS.res[s1]], inc=True)
                op(PE, lambda: nc.tensor.matmul(banks[s2][:], ones[:], SQ.tiles[q2][:], start=(c == 0), stop=(c == 7)),
                   reads=[R_ones, SQ.res[q2]], writes=[PS.res[s2]], inc=True)
                SQ.free(q1); SQ.free(q2)

        def stage_C(b):
            xb = X[b % 3]; rx = RX[b % 3]
            drain()
            if b not in ln_banks:
                assert b in seq_ends
                ln_banks[b] = (PS.alloc(), PS.alloc())
                for c in range(8):
                    ln_prep_chunk(b, c, None)
                    ln_flush(2)
                ln_flush(0)
            s1, s2 = ln_banks.pop(b)
            mean = TP.alloc(); var = TP.alloc()
            op(ACT, lambda: nc.scalar.activation(out=TP.tiles[var][:], in_=banks[s1][:], func=AF.Square, scale=1.0 / D),
               reads=[PS.res[s1]], writes=[TP.res[var]])
            op(ACT, lambda: nc.scalar.mul(out=TP.tiles[mean][:], in_=banks[s1][:], mul=1.0 / D), reads=[PS.res[s1]], writes=[TP.res[mean]])
            PS.free(s1)
            op(DVE, lambda: nc.vector.scalar_tensor_tensor(out=TP.tiles[var][:], in0=banks[s2][:], scalar=1.0 / D, in1=TP.tiles[var][:],
                                                           op0=ALU.mult, op1=ALU.subtract),
               reads=[PS.res[s2], TP.res[var]], writes=[TP.res[var]])
            PS.free(s2)
            op(ACT, lambda: nc.scalar.activation(out=TP.tiles[var][:], in_=TP.tiles[var][:], func=AF.Ln, bias=LN_EPS, scale=1.0),
               reads=[TP.res[var]], writes=[TP.res[var]])
            op(ACT, lambda: nc.scalar.activation(out=TP.tiles[var][:], in_=TP.tiles[var][:], func=AF.Exp, scale=-0.5),
               reads=[TP.res[var]], writes=[TP.res[var]])
            for c in range(8):
                op(DVE, lambda: nc.vector.tensor_tensor(out=C[:, c, :], in0=C[:, c, :], in1=TP.tiles[mean][:], op=ALU.subtract),
                   reads=[RC[c], TP.res[mean]], writes=[RC[c]])
                op(DVE, lambda: nc.vector.tensor_tensor(out=C[:, c, :], in0=C[:, c, :], in1=TP.tiles[var][:], op=ALU.mult),
                   reads=[RC[c], TP.res[var]], writes=[RC[c]])
                op(ACT, lambda: nc.scalar.activation(out=H[:, c, :], in_=C[:, c, :], func=AF.Silu, bias=pv[:, c, PV_LNB:PV_LNB + 1],
                                                     scale=pv[:, c, PV_LNG:PV_LNG + 1]),
                   reads=[RC[c], R_pv], writes=[RH[c]])
            TP.free(mean); TP.free(var)
            hk = [H[:, kc, :] for kc in range(8)]
            for t in range(2):
                wt, rw = wget(PW2(t))
                bks = [PS.alloc() for _ in range(4)]
                grp = [(bks[oc], [wt[:, kc * 512 + oc * 128: kc * 512 + (oc + 1) * 128] for kc in range(8)], hk, [rw], RH) for oc in range(4)]
                if t == 0:
                    mm_multi(grp)
                else:
                    for g in grp:
                        mm_group(g[0], g[1], g[2], reads=g[3], per=g[4])
                for oc in range(4):
                    o = 4 * t + oc; bk = bks[oc]
                    op(DVE, lambda: nc.vector.scalar_tensor_tensor(out=xb[:, o, :], in0=banks[bk][:], scalar=pv[:, o, PV_B2:PV_B2 + 1],
                                                                   in1=xb[:, o, :], op0=ALU.add, op1=ALU.add),
                       reads=[PS.res[bk], rx[o], R_pv], writes=[rx[o]])
                    PS.free(bk)
            ffn(3, xb, rx)
            warm_sqrt()
            sbank = PS.alloc()
            for c in range(8):
                q = SQ.alloc()
                op(ACT, lambda: nc.scalar.activation(out=SQ.tiles[q][:], in_=xb[:, c, :], func=AF.Square), reads=[rx[c]], writes=[SQ.res[q]])
                op(PE, lambda: nc.tensor.matmul(banks[sbank][:], ones[:], SQ.tiles[q][:], start=(c == 0), stop=(c == 7)),
                   reads=[R_ones, SQ.res[q]], writes=[PS.res[sbank]], inc=True)
                SQ.free(q)
            r = rstd_from_bank(sbank, RMS_EPS)
            for c in range(8):
                op(DVE, lambda: nc.vector.scalar_tensor_tensor(out=C[:, c, :], in0=xb[:, c, :], scalar=pv[:, c, PV_FINAL:PV_FINAL + 1],
                                                               in1=TP.tiles[r][:], op0=ALU.mult, op1=ALU.mult),
                   reads=[rx[c], TP.res[r], R_pv], writes=[RC[c]])
            TP.free(r)

        def stage_C_out(b):
            for s4 in range(4):
                xo = XOUT[s4 % 2]; rxo = R_XOUT[s4 % 2]
                for half in range(2):
                    bk = PS.alloc()
                    for cc in range(4):
                        c = half * 4 + cc
                        op(PE, lambda: nc.tensor.transpose(banks[bk][:, cc * 128:(cc + 1) * 128], C[:, c, s4 * 128:(s4 + 1) * 128], ident[:]),
                           reads=[RC[c], R_ident], writes=[PS.res[bk]], inc=(cc == 3))
                    op(ACT, lambda: nc.scalar.copy(out=xo[:, half * 512:(half + 1) * 512], in_=banks[bk][:]),
                       reads=[PS.res[bk]], writes=[rxo])
                    PS.free(bk)
                r0 = b * T + s4 * 128
                op(POOL, lambda: nc.gpsimd.dma_start(out=y_d[r0:r0 + 128, :], in_=xo[:]), reads=[rxo], chan=c_xout[s4 % 2])

        load_x_early(0); load_x_late(0)
        if nb > 1:
            load_x_early(1)
        stage_A_head(0)
        for s in range(nb + 2):
            if 0 <= s - 2 < nb:
                enqueue_conv(s - 2)
            if s < nb:
                stage_A(s)
            if 0 <= s - 1 < nb:
                stage_B(s - 1)
            if 0 <= s - 2 < nb:
                stage_C(s - 2)
            if s == 1:
                cv_finish()
            if s + 1 < nb:
                load_x_late(s + 1)
                stage_A_head(s + 1)
            if 0 <= s - 2 < nb:
                stage_C_out(s - 2)
        drain()
        assert wstate["used"] == len(plan)
        for c in c_xout:
            nc.gpsimd.wait_ge(c.sem, c.count)
        S.barrier()
    return nc


def make_host_inputs(inputs):
    f = lambda a: np.ascontiguousarray(np.asarray(a, dtype=np.float32))
    pvec = np.zeros((NPV, D), np.float32)
    pvec[0:4] = f(inputs["ffn_norm"]).reshape(4, D)
    pvec[4:6] = f(inputs["mix_norm"]).reshape(2, D)
    pvec[6] = f(inputs["final_norm"]).reshape(D)
    pvec[7:9] = f(inputs["c_b_pw1"]).reshape(2, D)
    pvec[9] = f(inputs["c_dw_b"]).reshape(D)
    pvec[10] = f(inputs["c_ln_g"]).reshape(D)
    pvec[11] = f(inputs["c_ln_b"]).reshape(D)
    pvec[12] = f(inputs["c_b_pw2"]).reshape(D)
    pvec[13:44] = f(inputs["c_dw_w"]).reshape(31, D)
    pvec[44:47, 0:512] = f(inputs["b_conv_w"]).reshape(3, 512)
    return {
        "wt32": host_weight_tiles(inputs),
        "pvec": pvec,
        "ws": f(inputs["a_spatial_w"]).reshape(8 * 128, 128),
        "bs": f(inputs["a_spatial_b"]).reshape(1, 8 * 128),
    }


def kernel(**inputs):
    xp = np.asarray(inputs["x_prompt"], dtype=np.float32)
    xs = np.asarray(inputs["x_sample"], dtype=np.float32)
    shared = make_host_inputs(inputs)
    nb = TOK_PER_CORE // T
    nc = build_nc(nb, seq_starts={0, 8, 16}, seq_ends={7, 15, 31})
    in_maps = []
    for i in range(NCORES):
        xc = np.concatenate([xp[2 * i].reshape(-1, D), xp[2 * i + 1].reshape(-1, D), xs[i].reshape(-1, D)], axis=0)
        m = dict(shared); m["x"] = np.ascontiguousarray(xc)
        in_maps.append(m)
    res = run_bass_kernel_spmd(nc, in_maps, core_ids=list(range(NCORES)))
    yp = np.empty_like(xp); ys = np.empty_like(xs)
    for i in range(NCORES):
        yc = res.results[i]["y"]
        yp[2 * i] = yc[0:4096]; yp[2 * i + 1] = yc[4096:8192]; ys[i] = yc[8192:16384]
    return (yp, ys)
```
